# Optimizing a Trainium2 kernel written in Bass

```python
import jax
import jax.numpy as jnp
from jax import lax
import numpy as np

D_MODEL = 1024
BATCH = 8
SEQ = 8192
DEPTH = 2

CTX_LEN = 256
GRID_W = 64
HEAD_DIM = 64
GLA_HEADS = 4
GLA_DK = 32
GLA_DV = 64
GLA_GATE_RANK = 16
GLA_GATE_TAU = 16.0
GLA_CHUNK = 64
GLB_HEADS = 8
GLB_KV_HEADS = 2
WIN_HEADS = 4
WIN_KV_HEADS = 2
WINDOW = 128
Q_BLOCK = 128
FFN_HIDDEN = 2816
ROPE_BASE = 10000.0
N_MOD = 9
EPS = 1e-6
MIX_WIDTH = GLA_HEADS * GLA_DV + GLB_HEADS * HEAD_DIM + WIN_HEADS * HEAD_DIM
IN_SPLITS = (GLA_HEADS * GLA_DK, GLA_HEADS * GLA_DK, GLA_HEADS * GLA_DV, GLA_HEADS * GLA_DV, 2 * GLA_GATE_RANK,
             GLB_HEADS * HEAD_DIM, GLB_KV_HEADS * HEAD_DIM, GLB_KV_HEADS * HEAD_DIM,
             WIN_HEADS * HEAD_DIM, WIN_KV_HEADS * HEAD_DIM, WIN_KV_HEADS * HEAD_DIM)
IN_WIDTH = sum(IN_SPLITS)

kernel_name = 'hymba_style_hybrid_dit_block'


def rms_norm(x, g):
    xf = x.astype(jnp.float32)
    y = xf * lax.rsqrt(jnp.mean(xf * xf, axis=-1, keepdims=True) + EPS)
    return (y * g.astype(jnp.float32)).astype(x.dtype)


def modulate(h, shift, scale):
    return h * (1 + scale) + shift


def swiglu(h, w_in, w_out):
    a, b = jnp.split(h @ w_in, 2, axis=-1)
    return (jax.nn.silu(a) * b) @ w_out


def rope_tables(n_tokens, dtype):
    rows = n_tokens // GRID_W
    row = jnp.repeat(jnp.arange(rows, dtype=jnp.float32), GRID_W)
    col = (jnp.arange(rows * GRID_W) % GRID_W).astype(jnp.float32)
    n_freq = HEAD_DIM // 4
    inv = jnp.power(ROPE_BASE, -jnp.arange(n_freq, dtype=jnp.float32) / n_freq)
    ang = jnp.concatenate([row[:, None] * inv, col[:, None] * inv], axis=-1)
    return jnp.cos(ang)[:, None, :].astype(dtype), jnp.sin(ang)[:, None, :].astype(dtype)


def apply_rope(x, cos, sin):
    x1, x2 = jnp.split(x, 2, axis=-1)
    return jnp.concatenate([x1 * cos - x2 * sin, x1 * sin + x2 * cos], axis=-1)


def gla_scan(q, k, v, g, s0):
    bsz, nh, length, dk = q.shape
    dv = v.shape[-1]
    n = length // GLA_CHUNK
    q = q.reshape(bsz, nh, n, GLA_CHUNK, dk)
    k = k.reshape(bsz, nh, n, GLA_CHUNK, dk)
    v = v.reshape(bsz, nh, n, GLA_CHUNK, dv)
    b = jnp.cumsum(g.reshape(bsz, nh, n, GLA_CHUNK, dk), axis=3)
    gam = b[:, :, :, -1:, :]
    q_in = q * jnp.exp(b)
    a = jnp.einsum('bhncd,bhnsd->bhncs', q_in, k * jnp.exp(-b))
    a = jnp.where(jnp.tril(jnp.ones((GLA_CHUNK, GLA_CHUNK), dtype=bool)), a, 0.0)
    o = jnp.einsum('bhncs,bhnsv->bhncv', a, v)
    ds = jnp.einsum('bhncd,bhncv->bhndv', k * jnp.exp(gam - b), v)
    decay = jnp.exp(gam[:, :, :, 0, :])

    def step(s, inp):
        dec, d = inp
        return dec[..., None] * s + d, s

    s_fin, s_prev = lax.scan(step, s0, (jnp.moveaxis(decay, 2, 0), jnp.moveaxis(ds, 2, 0)))
    o = o + jnp.einsum('bhncd,nbhdv->bhncv', q_in, s_prev)
    return o.reshape(bsz, nh, length, dv), s_fin


def gla_bidir(q, k, v, gf, gb, sf0, sb0):
    of, sf = gla_scan(q, k, v, gf, sf0)
    fl = lambda t: jnp.flip(t, axis=2)
    ob, sb = gla_scan(fl(q), fl(k), fl(v), fl(gb), sb0)
    return of + fl(ob), sf, sb


def gla_prepare(q, k, v, gd, wg_f, bg_f, wg_b, bg_b):
    bsz, length, _ = q.shape

    def heads(t, d):
        return t.astype(jnp.float32).reshape(bsz, length, GLA_HEADS, d).transpose(0, 2, 1, 3)

    gdf, gdb = jnp.split(gd.astype(jnp.float32), 2, axis=-1)
    gf = jax.nn.log_sigmoid(gdf @ wg_f.astype(jnp.float32) + bg_f.astype(jnp.float32)) / GLA_GATE_TAU
    gb = jax.nn.log_sigmoid(gdb @ wg_b.astype(jnp.float32) + bg_b.astype(jnp.float32)) / GLA_GATE_TAU
    return (heads(q, GLA_DK) * GLA_DK ** -0.5, heads(k, GLA_DK), heads(v, GLA_DV),
            heads(gf, GLA_DK), heads(gb, GLA_DK))


def gla_output(o, r, gain):
    bsz, nh, length, dv = o.shape
    o = rms_norm(o.transpose(0, 2, 1, 3), gain)
    gate = jax.nn.silu(r.astype(jnp.float32)).reshape(bsz, length, nh, dv)
    return (o * gate).reshape(bsz, length, nh * dv).astype(r.dtype)


def qk_prep(q, k, q_gain, k_gain, n_q, n_kv, rope):
    bsz, length, _ = q.shape
    q = rms_norm(q.reshape(bsz, length, n_q, HEAD_DIM), q_gain)
    k = rms_norm(k.reshape(bsz, length, n_kv, HEAD_DIM), k_gain)
    if rope is not None:
        cos, sin = rope
        q = apply_rope(q, cos, sin)
        k = apply_rope(k, cos, sin)
    q = q * HEAD_DIM ** -0.5
    return q.reshape(bsz, length, n_kv, n_q // n_kv, HEAD_DIM), k


def dense_attn(q, k, v, sink):
    bsz, nq, n_kv, grp, _ = q.shape
    s = jnp.einsum('bqhgd,bkhd->bhgqk', q, k).astype(jnp.float32)
    if sink is not None:
        snk = jnp.broadcast_to(sink.astype(jnp.float32).reshape(1, n_kv, grp, 1, 1), s.shape[:-1] + (1,))
        s = jnp.concatenate([s, snk], axis=-1)
    p = jax.nn.softmax(s, axis=-1)
    if sink is not None:
        p = p[..., :-1]
    o = jnp.einsum('bhgqk,bkhd->bqhgd', p.astype(v.dtype), v)
    return o.reshape(bsz, nq, -1)


def global_attn_latent(q, k, v):
    bsz, length = q.shape[:2]
    nb = length // Q_BLOCK
    qb = q.reshape((bsz, nb, Q_BLOCK) + q.shape[2:]).swapaxes(0, 1)
    o = lax.map(lambda qi: dense_attn(qi, k, v, None), qb)
    return o.swapaxes(0, 1).reshape(bsz, length, -1)


def window_attn_latent(q, k, v, kc, vc, sink):
    bsz, length, n_kv, grp, hd = q.shape
    nb = length // Q_BLOCK
    qb = q.reshape(bsz, nb, Q_BLOCK, n_kv, grp, hd).swapaxes(0, 1)

    def band(t):
        tp = jnp.pad(t, ((0, 0), (Q_BLOCK, Q_BLOCK), (0, 0), (0, 0))).reshape(bsz, nb + 2, Q_BLOCK, n_kv, hd)
        return jnp.concatenate([tp[:, :-2], tp[:, 1:-1], tp[:, 2:]], axis=2).swapaxes(0, 1)

    rel = jnp.arange(3 * Q_BLOCK)[None, :] - jnp.arange(Q_BLOCK)[:, None]
    in_win = (rel >= Q_BLOCK - WINDOW) & (rel <= Q_BLOCK + WINDOW)
    kpos = (jnp.arange(nb)[:, None] - 1) * Q_BLOCK + jnp.arange(3 * Q_BLOCK)[None, :]
    mask = in_win[None] & ((kpos >= 0) & (kpos < length))[:, None, :]
    snk = jnp.broadcast_to(sink.astype(jnp.float32).reshape(1, n_kv, grp, 1, 1), (bsz, n_kv, grp, Q_BLOCK, 1))
    n_loc = 3 * Q_BLOCK
    n_ctx = kc.shape[1]

    def blk(args):
        qi, ki, vi, mi = args
        s_loc = jnp.einsum('bqhgd,bkhd->bhgqk', qi, ki).astype(jnp.float32)
        s_loc = jnp.where(mi, s_loc, -jnp.inf)
        s_ctx = jnp.einsum('bqhgd,bkhd->bhgqk', qi, kc).astype(jnp.float32)
        p = jax.nn.softmax(jnp.concatenate([s_loc, s_ctx, snk], axis=-1), axis=-1).astype(vi.dtype)
        o = (jnp.einsum('bhgqk,bkhd->bqhgd', p[..., :n_loc], vi)
             + jnp.einsum('bhgqk,bkhd->bqhgd', p[..., n_loc:n_loc + n_ctx], vc))
        return o.reshape(bsz, Q_BLOCK, -1)

    o = lax.map(blk, (qb, band(k), band(v), mask))
    return o.swapaxes(0, 1).reshape(bsz, length, -1)


def mixer(h, hc, rope, w_in, w_out, wg_f, bg_f, wg_b, bg_b, gla_gain,
          glb_qg, glb_kg, win_qg, win_kg, sink, need_ctx):
    bsz, length, _ = h.shape
    n_ctx = hc.shape[1]
    idx = np.cumsum(IN_SPLITS)[:-1]
    aq, ak, av, ar, ad, gq, gk, gv, wq, wk, wv = jnp.split(h @ w_in, idx, axis=-1)
    aqc, akc, avc, arc, adc, gqc, gkc, gvc, wqc, wkc, wvc = jnp.split(hc @ w_in, idx, axis=-1)

    s_zero = jnp.zeros((bsz, GLA_HEADS, GLA_DK, GLA_DV), jnp.float32)
    oc_a, s_f, s_b = gla_bidir(*gla_prepare(aqc, akc, avc, adc, wg_f, bg_f, wg_b, bg_b), s_zero, s_zero)
    o_a, _, _ = gla_bidir(*gla_prepare(aq, ak, av, ad, wg_f, bg_f, wg_b, bg_b), s_f, s_b)

    q_g, k_g = qk_prep(gq, gk, glb_qg, glb_kg, GLB_HEADS, GLB_KV_HEADS, rope)
    qc_g, kc_g = qk_prep(gqc, gkc, glb_qg, glb_kg, GLB_HEADS, GLB_KV_HEADS, None)
    v_g = gv.reshape(bsz, length, GLB_KV_HEADS, HEAD_DIM)
    vc_g = gvc.reshape(bsz, n_ctx, GLB_KV_HEADS, HEAD_DIM)
    o_b = global_attn_latent(q_g, jnp.concatenate([k_g, kc_g], axis=1), jnp.concatenate([v_g, vc_g], axis=1))

    q_w, k_w = qk_prep(wq, wk, win_qg, win_kg, WIN_HEADS, WIN_KV_HEADS, rope)
    qc_w, kc_w = qk_prep(wqc, wkc, win_qg, win_kg, WIN_HEADS, WIN_KV_HEADS, None)
    v_w = wv.reshape(bsz, length, WIN_KV_HEADS, HEAD_DIM)
    vc_w = wvc.reshape(bsz, n_ctx, WIN_KV_HEADS, HEAD_DIM)
    o_c = window_attn_latent(q_w, k_w, v_w, kc_w, vc_w, sink)

    out = jnp.concatenate([gla_output(o_a, ar, gla_gain), o_b, o_c], axis=-1) @ w_out
    if not need_ctx:
        return out, None
    oc = jnp.concatenate([gla_output(oc_a, arc, gla_gain),
                          dense_attn(qc_g, kc_g, vc_g, None),
                          dense_attn(qc_w, kc_w, vc_w, sink)], axis=-1) @ w_out
    return out, oc


def setup_inputs(seed: int = 0) -> dict:
    key = jax.random.key(seed)
    ks = jax.random.split(key, 26)
    f32 = jnp.float32

    def nrm(k, shape, scale):
        return jax.random.normal(k, shape, f32) * scale

    def gain(k, shape):
        return 1.0 + 0.02 * jax.random.normal(k, shape, f32)

    return {
        'x': nrm(ks[0], (BATCH, SEQ, D_MODEL), 1.0),
        'c': nrm(ks[1], (BATCH, D_MODEL), 1.0),
        'ctx': nrm(ks[2], (BATCH, CTX_LEN, D_MODEL), 1.0),
        'c_ctx': nrm(ks[3], (D_MODEL,), 1.0),
        'mod_w': nrm(ks[4], (DEPTH, D_MODEL, N_MOD * D_MODEL), D_MODEL ** -0.5),
        'mod_b': nrm(ks[5], (DEPTH, N_MOD * D_MODEL), 0.02),
        'norm_ffn1': gain(ks[6], (DEPTH, D_MODEL)),
        'ffn1_w_in': nrm(ks[7], (DEPTH, D_MODEL, 2 * FFN_HIDDEN), D_MODEL ** -0.5),
        'ffn1_w_out': nrm(ks[8], (DEPTH, FFN_HIDDEN, D_MODEL), FFN_HIDDEN ** -0.5),
        'norm_mix': gain(ks[9], (DEPTH, D_MODEL)),
        'mix_w_in': nrm(ks[10], (DEPTH, D_MODEL, IN_WIDTH), D_MODEL ** -0.5),
        'mix_w_out': nrm(ks[11], (DEPTH, MIX_WIDTH, D_MODEL), MIX_WIDTH ** -0.5),
        'gla_wg_f': nrm(ks[12], (DEPTH, GLA_GATE_RANK, GLA_HEADS * GLA_DK), GLA_GATE_RANK ** -0.5),
        'gla_bg_f': nrm(ks[13], (DEPTH, GLA_HEADS * GLA_DK), 0.02),
        'gla_wg_b': nrm(ks[14], (DEPTH, GLA_GATE_RANK, GLA_HEADS * GLA_DK), GLA_GATE_RANK ** -0.5),
        'gla_bg_b': nrm(ks[15], (DEPTH, GLA_HEADS * GLA_DK), 0.02),
        'gla_out_norm': gain(ks[16], (DEPTH, GLA_DV)),
        'glb_q_norm': gain(ks[17], (DEPTH, HEAD_DIM)),
        'glb_k_norm': gain(ks[18], (DEPTH, HEAD_DIM)),
        'win_q_norm': gain(ks[19], (DEPTH, HEAD_DIM)),
        'win_k_norm': gain(ks[20], (DEPTH, HEAD_DIM)),
        'win_sink': nrm(ks[21], (DEPTH, WIN_HEADS), 0.5),
        'norm_ffn2': gain(ks[22], (DEPTH, D_MODEL)),
        'ffn2_w_in': nrm(ks[23], (DEPTH, D_MODEL, 2 * FFN_HIDDEN), D_MODEL ** -0.5),
        'ffn2_w_out': nrm(ks[24], (DEPTH, FFN_HIDDEN, D_MODEL), FFN_HIDDEN ** -0.5),
    }


def reference(x, c, ctx, c_ctx, mod_w, mod_b, norm_ffn1, ffn1_w_in, ffn1_w_out, norm_mix, mix_w_in, mix_w_out,
              gla_wg_f, gla_bg_f, gla_wg_b, gla_bg_b, gla_out_norm, glb_q_norm, glb_k_norm,
              win_q_norm, win_k_norm, win_sink, norm_ffn2, ffn2_w_in, ffn2_w_out):
    length = x.shape[1]
    rope = rope_tables(length, x.dtype)
    xc = ctx
    sc = jax.nn.silu(c)
    scc = jax.nn.silu(c_ctx)
    for l in range(DEPTH):
        need_ctx = l < DEPTH - 1
        ml = jnp.split((sc @ mod_w[l] + mod_b[l])[:, None, :], N_MOD, axis=-1)
        mc = jnp.split((scc @ mod_w[l] + mod_b[l])[None, None, :], N_MOD, axis=-1)
        x = x + 0.5 * ml[2] * swiglu(modulate(rms_norm(x, norm_ffn1[l]), ml[0], ml[1]), ffn1_w_in[l], ffn1_w_out[l])
        xc = xc + 0.5 * mc[2] * swiglu(modulate(rms_norm(xc, norm_ffn1[l]), mc[0], mc[1]), ffn1_w_in[l], ffn1_w_out[l])
        h = modulate(rms_norm(x, norm_mix[l]), ml[3], ml[4])
        hc = modulate(rms_norm(xc, norm_mix[l]), mc[3], mc[4])
        o, oc = mixer(h, hc, rope, mix_w_in[l], mix_w_out[l], gla_wg_f[l], gla_bg_f[l], gla_wg_b[l], gla_bg_b[l],
                      gla_out_norm[l], glb_q_norm[l], glb_k_norm[l], win_q_norm[l], win_k_norm[l], win_sink[l],
                      need_ctx)
        x = x + ml[5] * o
        x = x + 0.5 * ml[8] * swiglu(modulate(rms_norm(x, norm_ffn2[l]), ml[6], ml[7]), ffn2_w_in[l], ffn2_w_out[l])
        if need_ctx:
            xc = xc + mc[5] * oc
            xc = xc + 0.5 * mc[8] * swiglu(modulate(rms_norm(xc, norm_ffn2[l]), mc[6], mc[7]),
                                           ffn2_w_in[l], ffn2_w_out[l])
    return x
```

```python
import contextlib
import numpy as np
import concourse.bass as bass
import concourse.mybir as mybir
from concourse.bass_utils import run_bass_kernel_spmd

F32 = mybir.dt.float32
BF16 = mybir.dt.bfloat16
AF = mybir.ActivationFunctionType
ALU = mybir.AluOpType
AX = mybir.AxisListType

D = 1024
CTX = 256
HID = 2816
NMOD = 9
INW = 2080
EPS = 1e-6
EPOCH = 30000
C_ID = 0
C_TRI = 128
C_IND = 640
C_WM = 642
C_GM = 898
C_BD = 1026
NCONST = 1032


class Tl:
    __slots__ = ("w", "r", "name")

    def __init__(self, name=""):
        self.w = None
        self.r = {}
        self.name = name


class Sched:
    ENGS = ("pe", "act", "dve", "pool", "sp")

    def __init__(self, nc, sem_pool):
        self.nc = nc
        self.free_sems = list(sem_pool)
        self.ops = {e: [] for e in self.ENGS}
        self.cnt = {e: 0 for e in self.ENGS}
        self.seen = {e: {} for e in self.ENGS}
        self.semh = {}
        self.semv = {}
        self.epoch = {e: 0 for e in self.ENGS}
        self.ninstr = 0
        for e in ("pe", "act", "dve", "pool"):
            self._new_epoch(e, first=True)

    def _alloc(self, key):
        h = self.free_sems.pop()
        self.semh[key] = h
        self.semv[key] = 0
        return key

    def _new_epoch(self, e, first=False):
        if not first:
            self.epoch[e] += 1
        self._alloc((e, self.epoch[e]))
        self.cnt[e] = 0

    def dma_sem(self, name):
        return self._alloc(("dma", name))

    def _wait(self, eng, key, val):
        if self.seen[eng].get(key, 0) >= val:
            return
        self.seen[eng][key] = val
        h = self.semh[key]
        self.ops[eng].append(lambda E, h=h, v=val: E.wait_ge(h, v))
        self.ninstr += 1

    def _deps(self, eng, reads, writes, ownkey):
        deps = {}
        raw = {}
        for t in reads:
            if t.w is not None:
                k, v = t.w
                if raw.get(k, 0) < v:
                    raw[k] = v
        for t in writes:
            if t.w is not None:
                k, v = t.w
                if deps.get(k, 0) < v:
                    deps[k] = v
            for k, v in t.r.items():
                if deps.get(k, 0) < v:
                    deps[k] = v
        for k, v in raw.items():
            if k == ownkey and eng == "pe":
                continue
            self._wait(eng, k, v)
        for k, v in deps.items():
            if k == ownkey and eng in ("pe", "sp", "pool_dma"):
                continue
            if k == ownkey and k[0] == "dma":
                continue
            self._wait(eng, k, v)

    def op(self, eng, fn, reads=(), writes=()):
        if self.cnt[eng] >= EPOCH:
            self._new_epoch(eng)
        own = (eng, self.epoch[eng])
        self._deps(eng, reads, writes, own)
        self.cnt[eng] += 1
        self.semv[own] = self.cnt[eng]
        h = self.semh[own]
        self.ops[eng].append(lambda E, fn=fn, h=h: fn(E).then_inc(h, 1))
        self.ninstr += 1
        dep = (own, self.cnt[eng])
        for t in reads:
            if t.r.get(own, 0) < dep[1]:
                t.r[own] = dep[1]
        for t in writes:
            t.w = dep
            t.r = {}

    def dma(self, semkey, out, in_, reads=(), writes=(), q="sp", **kw):
        self._deps(q, reads, writes, semkey)
        self.semv[semkey] += 16
        v = self.semv[semkey]
        h = self.semh[semkey]
        self.ops[q].append(lambda E, out=out, in_=in_, h=h, kw=kw: E.dma_start(out=out, in_=in_, **kw).then_inc(h, 16))
        self.ninstr += 1
        dep = (semkey, v)
        for t in reads:
            if t.r.get(semkey, 0) < v:
                t.r[semkey] = v
        for t in writes:
            t.w = dep
            t.r = {}

    def barrier(self):
        for e in self.ENGS:
            for k, v in self.semv.items():
                if v > 0:
                    self._wait(e, k, v)

    def final_wait(self, eng="sp"):
        for k, v in self.semv.items():
            if v > 0:
                self._wait(eng, k, v)


class Arena:
    def __init__(self, base_ap, nwords):
        self.base = base_ap
        self.n = nwords
        self.off = 0

    def alloc(self, nelem, dtype=F32):
        words = nelem if dtype == F32 else (nelem + 1) // 2
        words = (words + 7) // 8 * 8
        assert self.off + words <= self.n, f"SBUF arena overflow {self.off}+{words}>{self.n}"
        a = self.base[:, self.off:self.off + words]
        self.off += words
        if dtype != F32:
            a = a.bitcast(dtype)[:, :nelem]
        else:
            a = a[:, :nelem]
        return a

    def mark(self):
        return self.off

    def release(self, m):
        self.off = m


class Cfg:
    def __init__(self, T=8192, depth=2, stop_after=None, debug=False, skip=()):
        self.skip = set(skip)
        self.T = T
        self.depth = depth
        self.stop_after = stop_after
        self.debug = debug


def build_nc(cfg):
    T = cfg.T
    nc = bass.Bass("TRN2", target_bir_lowering=False)
    dr = {}

    def din(name, shape, dt=F32):
        dr[name] = nc.dram_tensor(name, list(shape), dt, kind="ExternalInput").ap()
        return dr[name]

    L = cfg.depth
    dk = dict(kind="ExternalOutput") if cfg.debug else {}
    x_in = din("x", [T, D])
    ctx_in = din("ctx", [CTX, D])
    cvec = din("cvec", [2, D])
    mod_w = din("mod_w", [L, D, NMOD * D])
    mod_b = din("mod_b", [L, NMOD * D])
    norm_ffn1 = din("norm_ffn1", [L, D])
    ffn1_w_in = din("ffn1_w_in", [L, D, 2 * HID])
    ffn1_w_out = din("ffn1_w_out", [L, HID, D])
    norm_ffn2 = din("norm_ffn2", [L, D])
    ffn2_w_in = din("ffn2_w_in", [L, D, 2 * HID])
    ffn2_w_out = din("ffn2_w_out", [L, HID, D])
    for nm, shp in (("norm_mix", [L, D]), ("mix_w_in", [L, D, INW]), ("mix_w_out", [L, D, D]),
                    ("gla_wg_f", [L, 16, 128]), ("gla_bg_f", [L, 128]), ("gla_wg_b", [L, 16, 128]), ("gla_bg_b", [L, 128]),
                    ("gla_out_norm", [L, 64]), ("glb_q_norm", [L, 64]), ("glb_k_norm", [L, 64]),
                    ("win_q_norm", [L, 64]), ("win_k_norm", [L, 64]), ("win_sink", [L, 4])):
        din(nm, shp)
    din("consts", [128, NCONST])
    din("rope", [T, 64])
    TC = T + CTX
    dr["qkT_d"] = nc.dram_tensor("qkT_d", [128, 8, TC], BF16, **dk).ap()
    dr["v_d"] = nc.dram_tensor("v_d", [TC, 512], BF16, **dk).ap()
    dr["gla_d"] = nc.dram_tensor("gla_d", [TC, 1024], F32, **dk).ap()
    dr["mixT_d"] = nc.dram_tensor("mixT_d", [128, 8, TC], BF16, **dk).ap()
    dr["of_d"] = nc.dram_tensor("of_d", [TC, 256], F32).ap()
    out_d = nc.dram_tensor("out", [T, D], F32, kind="ExternalOutput").ap()
    dk = dict(kind="ExternalOutput") if cfg.debug else {}
    xs = [nc.dram_tensor(f"xs{i}", [T, D], F32, **dk).ap() for i in range(2)]
    xcs = [nc.dram_tensor(f"xcs{i}", [CTX, D], F32).ap() for i in range(2)]
    dr["gates_d"] = nc.dram_tensor("gates_d", [L * 6, D], F32).ap()

    NW = 52000
    with contextlib.ExitStack() as es:
        arena_t = es.enter_context(nc.sbuf_tensor("arena", [128, NW], F32))
        bigs = [es.enter_context(nc.psum_tensor(f"pbig{i}", [128, 1024], F32)) for i in range(4)]
        banks = [bigs[i // 2][:, (i % 2) * 512:(i % 2 + 1) * 512] for i in range(8)]
        sems = [es.enter_context(nc.semaphore(f"s{i}")) for i in range(96)]
        S = Sched(nc, sems)
        A = Arena(arena_t[:], NW)
        P = Prog(nc, cfg, S, A, banks, dr, xs, xcs, out_d)
        P.bigs = bigs
        P.build()
        S.final_wait("sp")
        block = es.enter_context(nc.Block())

        @block.tensor
        def _(E):
            for f in S.ops["pe"]:
                f(E)

        @block.scalar
        def _(E):
            for f in S.ops["act"]:
                f(E)

        @block.vector
        def _(E):
            for f in S.ops["dve"]:
                f(E)

        @block.gpsimd
        def _(E):
            for f in S.ops["pool"]:
                f(E)

        @block.sync
        def _(E):
            for f in S.ops["sp"]:
                f(E)
    nc._n_instr = S.ninstr
    return nc


class Prog:
    def __init__(self, nc, cfg, S, A, banks, dr, xs, xcs, out_d):
        self.nc, self.cfg, self.S, self.A = nc, cfg, S, A
        self.banks = banks
        self.bank_t = [Tl(f"bank{i}") for i in range(8)]
        self.dr = dr
        self.xs, self.xcs, self.out_d = xs, xcs, out_d
        self.T = cfg.T
        self.dma_sems = {}

    def dsem(self, name):
        if name not in self.dma_sems:
            self.dma_sems[name] = self.S.dma_sem(name)
        return self.dma_sems[name]

    def build(self):
        S, A, nc = self.S, self.A, self.nc
        cfg = self.cfg
        self.consts = A.alloc(NCONST, F32)
        self.consts_t = Tl("consts")
        S.dma(self.dsem("const_id"), self.consts, self.dr["consts"], writes=[self.consts_t])
        self.ident = A.alloc(128, BF16)
        self.ident_t = Tl("ident")
        S.op("dve", lambda E: E.tensor_copy(out=self.ident, in_=self.consts[:, C_ID:C_ID + 128]), reads=[self.consts_t], writes=[self.ident_t])
        self.ones_f = A.alloc(128, F32)
        self.ones_t = Tl("ones")
        S.op("dve", lambda E: E.memset(self.ones_f, 1.0), writes=[self.ones_t])
        self.sc = A.alloc(16, F32)
        self.sc_t = Tl("sc")
        craw = A.alloc(16, F32)
        craw_t = Tl()
        S.dma(self.dsem("const_c"), craw.rearrange("p (r j) -> p r j", r=2),
              self.dr["cvec"].rearrange("r (p j) -> p r j", j=8), writes=[craw_t])
        S.op("act", lambda E: E.activation(out=self.sc, in_=craw, func=AF.Silu), reads=[craw_t], writes=[self.sc_t])
        self.modcols = []
        self.modcols_t = []
        self.gateB_t = []
        for l in range(cfg.depth):
            self.modcols.append(A.alloc(6 * 8 * 2, F32))
            self.modcols_t.append(Tl(f"modcols{l}"))
            self.gateB_t.append(Tl(f"gateB{l}"))
        self.persist_mark = A.mark()
        xs, xcs = self.xs, self.xcs
        cur_x, cur_xc = self.dr["x"], self.dr["ctx"]
        for l in range(cfg.depth):
            last = (l == cfg.depth - 1)
            need_ctx = not last
            self.mod_phase(l)
            S.barrier()
            A.release(self.persist_mark)
            if cfg.stop_after == ("ffn1", l):
                self.ffn_phase(l, 0, cur_x, cur_xc, self.out_d, xcs[0])
                return
            sk = cfg.skip
            self.ffn_phase(l, 0, cur_x, cur_xc, xs[0], xcs[0])
            if "inproj" not in sk:
                self.inproj_phase(l, xs[0], xcs[0])
            if "gattn" not in sk:
                self.gattn_phase(l, need_ctx)
            if "wattn" not in sk:
                self.wattn_phase(l, need_ctx)
            if "gla" not in sk:
                self.gla_phase(l, need_ctx)
            self.outproj_phase(l, xs[0], xcs[0], xs[1], xcs[1], need_ctx)
            dst = self.out_d if last else xs[0]
            self.ffn_phase(l, 1, xs[1], xcs[1], dst, xcs[0], do_ctx=need_ctx)
            cur_x, cur_xc = xs[0], xcs[0]

    def mod_phase(self, l):
        S, A, nc = self.S, self.A, self.nc
        m = A.mark()
        W = [A.alloc(8 * 1024, F32) for _ in range(2)]
        W_t = [Tl(f"modW{i}") for i in range(2)]
        brow = A.alloc(NMOD * D, F32)
        brow_t = Tl("brow")
        S.dma(self.dsem("modb"), brow[0:1, :], self.dr["mod_b"][l:l + 1, :], writes=[brow_t])
        scB = A.alloc(2 * 8 * 128, F32)
        scB_t = Tl("scB")
        grow = [A.alloc(512, F32) for _ in range(2)]
        grow_t = [Tl("grow0"), Tl("grow1")]
        ones = self.ones_f
        for r in range(2):
            for j in range(8):
                o = scB[:, (r * 8 + j) * 128:(r * 8 + j + 1) * 128]
                s1 = self.sc[:, r * 8 + j:r * 8 + j + 1]
                S.op("dve", lambda E, o=o, s1=s1: E.tensor_scalar(out=o, in0=ones, scalar1=s1, scalar2=None, op0=ALU.mult),
                     reads=[self.sc_t, self.ones_t], writes=[scB_t])
        colvec = {0: 0, 1: 1, 3: 2, 4: 3, 6: 4, 7: 5}
        gatevec = {2: 0, 5: 1, 8: 2}
        modw = self.dr["mod_w"]
        for v in range(NMOD):
            wb = W[v % 2]
            wt = W_t[v % 2]
            S.dma(self.dsem(f"modw{v % 2}"), wb.rearrange("p (j n) -> p j n", j=8),
                  modw[l, :, v * D:(v + 1) * D].rearrange("(p j) n -> p j n", j=8), writes=[wt])
            if v in colvec:
                ci = colvec[v]
                bk = self.banks[0]
                bt = self.bank_t[0]
                for c in range(8):
                    for j in range(8):
                        lhsT = wb[:, j * 1024 + c * 128: j * 1024 + (c + 1) * 128]
                        rhs = self.sc.rearrange("p (r j) -> p j r", r=2)[:, j, :]
                        S.op("pe", lambda E, o=bk[:, c * 2:c * 2 + 2], lhsT=lhsT, rhs=rhs, j=j:
                             E.matmul(o, lhsT, rhs, start=(j == 0), stop=False),
                             reads=[wt, self.sc_t], writes=[bt])
                    lb = brow[0:1, v * D + c * 128: v * D + (c + 1) * 128]
                    S.op("pe", lambda E, o=bk[:, c * 2:c * 2 + 2], lb=lb:
                         E.matmul(o, lb, ones[0:1, 0:2], start=False, stop=True),
                         reads=[brow_t, self.ones_t], writes=[bt])
                dst = self.modcols[l][:, ci * 16:(ci + 1) * 16]
                S.op("dve", lambda E, dst=dst, bk=bk: E.tensor_copy(out=dst, in_=bk[:, 0:16]),
                     reads=[bt], writes=[self.modcols_t[l]])
            else:
                gi = gatevec[v]
                for r in range(2):
                    for hf in range(2):
                        bi = 1 + (r * 2 + hf)
                        bk, bt = self.banks[bi], self.bank_t[bi]
                        for j in range(8):
                            lhsT = scB[:, (r * 8 + j) * 128:(r * 8 + j + 1) * 128]
                            rhs = wb[:, j * 1024 + hf * 512: j * 1024 + (hf + 1) * 512]
                            S.op("pe", lambda E, bk=bk, lhsT=lhsT, rhs=rhs, j=j:
                                 E.matmul(bk[:, :], lhsT, rhs, start=(j == 0), stop=False),
                                 reads=[wt, scB_t], writes=[bt])
                        rb = brow[0:1, v * D + hf * 512: v * D + (hf + 1) * 512]
                        S.op("pe", lambda E, bk=bk, rb=rb: E.matmul(bk[:, :], ones[0:1, :], rb, start=False, stop=True),
                             reads=[brow_t, self.ones_t], writes=[bt])
                        gb = grow[(r * 2 + hf) % 2]
                        gbt = grow_t[(r * 2 + hf) % 2]
                        sc_ = 0.5 if v in (2, 8) else 1.0
                        S.op("act", lambda E, gb=gb, bk=bk, sc_=sc_: E.activation(out=gb, in_=bk[:, :], func=AF.Copy, scale=sc_),
                             reads=[bt], writes=[gbt])
                        row = l * 6 + gi * 2 + r
                        S.dma(self.dsem(f"gst{(r * 2 + hf) % 2}"), self.dr["gates_d"][row:row + 1, hf * 512:(hf + 1) * 512], gb[0:1, :],
                              reads=[gbt], writes=[self.gateB_t[l]])
        A.release(m)

    def make_GS(self, l, norm_name, vec_shift_ci, vec_scale_ci):
        S, A = self.S, self.A
        gs = A.alloc(32, F32)
        gs_t = Tl("gs")
        gcol = A.alloc(8, F32)
        gcol_t = Tl("gcol")
        with self.nc.allow_non_contiguous_dma(reason="tiny norm gain column load"):
            pass
        S.dma(self.dsem("gcol"), gcol, self.dr[norm_name][l, :].rearrange("(j p) -> p j", p=128), writes=[gcol_t],
              allow_slow_non_contiguous=True)
        mc = self.modcols[l].rearrange("p (v c r) -> p v r c", v=6, c=8, r=2)
        for r in range(2):
            go = gs[:, (r * 2) * 8:(r * 2 + 1) * 8]
            so = gs[:, (r * 2 + 1) * 8:(r * 2 + 2) * 8]
            S.op("dve", lambda E, go=go, r=r: E.scalar_tensor_tensor(out=go, in0=mc[:, vec_scale_ci, r, :], scalar=1.0, in1=gcol,
                                                                    op0=ALU.add, op1=ALU.mult),
                 reads=[self.modcols_t[l], gcol_t], writes=[gs_t])
            S.op("dve", lambda E, so=so, r=r: E.tensor_copy(out=so, in_=mc[:, vec_shift_ci, r, :]),
                 reads=[self.modcols_t[l]], writes=[gs_t])
        return gs, gs_t

    def norm_to_hT(self, xt, xt_t, nt, gs, gs_t, r, xn, xn_t, hT, hT_t, ss, ss_t, junk, junk_t, tbanks, part="both"):
        S = self.S
        N = nt * 128
        if part in ("both", "norm"):
            self._norm_part(xt, xt_t, nt, xn, xn_t, ss, ss_t, junk, junk_t)
        if part in ("both", "tr"):
            self._tr_part(nt, gs, gs_t, r, xn, xn_t, hT, hT_t, tbanks)

    def _norm_part(self, xt, xt_t, nt, xn, xn_t, ss, ss_t, junk, junk_t):
        S = self.S
        N = nt * 128
        for s in range(nt):
            xin = xt[:, s * 1024:(s + 1) * 1024]
            S.op("act", lambda E, xin=xin, s=s: E.activation(out=junk, in_=xin, func=AF.Square, accum_out=ss[:, s:s + 1]),
                 reads=[xt_t], writes=[junk_t, ss_t])
        S.op("dve", lambda E: E.tensor_scalar(out=ss[:, 4:4 + nt], in0=ss[:, 0:nt], scalar1=1.0 / D, scalar2=EPS,
                                              op0=ALU.mult, op1=ALU.add), reads=[ss_t], writes=[ss_t])
        S.op("act", lambda E: E.activation(out=ss[:, 8:8 + nt], in_=ss[:, 4:4 + nt], func=AF.Sqrt), reads=[ss_t], writes=[ss_t])
        S.op("dve", lambda E: E.reciprocal(out=ss[:, 12:12 + nt], in_=ss[:, 8:8 + nt]), reads=[ss_t], writes=[ss_t])
        for s in range(nt):
            xin = xt[:, s * 1024:(s + 1) * 1024]
            xo = xn[:, s * 1024:(s + 1) * 1024]
            eng = "dve" if s % 2 == 0 else "pool"
            S.op(eng, lambda E, xin=xin, xo=xo, s=s: E.tensor_scalar(out=xo, in0=xin, scalar1=ss[:, 12 + s:13 + s], scalar2=None,
                                                                   op0=ALU.mult), reads=[xt_t, ss_t], writes=[xn_t])

    def _tr_part(self, nt, gs, gs_t, r, xn, xn_t, hT, hT_t, tbanks):
        S = self.S
        N = nt * 128
        for j in range(8):
            bi = tbanks[j % len(tbanks)]
            bk = self.banks[bi][:, :].bitcast(BF16)
            bt = self.bank_t[bi]
            for s in range(nt):
                S.op("pe", lambda E, bk=bk, s=s, j=j: E.transpose(out=bk[:, s * 128:(s + 1) * 128],
                                                                 in_=xn[:, s * 1024 + j * 128: s * 1024 + (j + 1) * 128],
                                                                 identity=self.ident),
                     reads=[xn_t, self.ident_t], writes=[bt])
            gcol = gs[:, (r * 2) * 8 + j:(r * 2) * 8 + j + 1]
            scol = gs[:, (r * 2 + 1) * 8 + j:(r * 2 + 1) * 8 + j + 1]
            S.op("act", lambda E, bk=bk, j=j, gcol=gcol, scol=scol: E.activation(out=hT[:, j * N:(j + 1) * N], in_=bk[:, 0:N],
                                                                               func=AF.Identity, bias=scol, scale=gcol),
                 reads=[bt, gs_t], writes=[hT_t])

    def ffn_phase(self, l, which, src_x, src_xc, dst_x, dst_xc, do_ctx=True):
        S, A, nc = self.S, self.A, self.nc
        T = self.T
        m = A.mark()
        wname = "ffn1" if which == 0 else "ffn2"
        Win = A.alloc(8 * 2 * HID, BF16)
        Win_t = Tl("Win")
        Wout = A.alloc(22 * D, BF16)
        Wout_t = Tl("Wout")
        w_in_d = self.dr[wname + "_w_in"][l].rearrange("(j p) n -> p j n", p=128)
        w_out_d = self.dr[wname + "_w_out"][l].rearrange("(j p) n -> p j n", p=128)
        for j in range(8):
            S.dma(self.dsem("wld_in"), Win[:, j * 2 * HID:(j + 1) * 2 * HID], w_in_d[:, j, :], writes=[Win_t], q="pool")
        for j in range(22):
            S.dma(self.dsem("wld_out"), Wout[:, j * D:(j + 1) * D], w_out_d[:, j, :], writes=[Wout_t], q="pool")
        if which == 0:
            gs, gs_t = self.make_GS(l, "norm_ffn1", 0, 1)
            gi = 0
        else:
            gs, gs_t = self.make_GS(l, "norm_ffn2", 4, 5)
            gi = 2
        NT = 2
        gB2 = A.alloc(2 * 1024, F32)
        gB2_t = Tl("gB2")
        for r in range(2):
            row = l * 6 + gi * 2 + r
            S.dma(self.dsem("gld"), gB2[:, r * 1024:(r + 1) * 1024], self.dr["gates_d"][row:row + 1, :].partition_broadcast(128),
                  reads=[self.gateB_t[l]], writes=[gB2_t])
        xts = [A.alloc(NT * 1024, F32) for _ in range(2)]
        xts_t = [Tl(f"xt{i}") for i in range(2)]
        xn = A.alloc(NT * 1024, BF16); xn_t = Tl("xn")
        hT = A.alloc(8 * NT * 128, BF16); hT_t = Tl("hT")
        g = A.alloc(22 * NT * 128, BF16); g_t = Tl("g")
        ss = A.alloc(16, F32); ss_t = Tl("ss")
        junk = A.alloc(1024, BF16); junk_t = Tl("junk")
        sa = [A.alloc(NT * 128, F32) for _ in range(2)]
        sa_t = [Tl(f"sa{i}") for i in range(2)]
        tmp = [A.alloc(512, F32) for _ in range(2)]
        tmp_t = [Tl(f"tmp{i}") for i in range(2)]
        tiles = []
        if do_ctx:
            tiles.append((src_xc, dst_xc, 0, 2, 1))
        for i in range(T // 256):
            tiles.append((src_x, dst_x, i * 256, 2, 0))
        def load(ti):
            src, dst, row0, nt, r = tiles[ti]
            b = ti % 2
            S.dma(self.dsem(f"xld{b}"), xts[b][:, 0:nt * 1024].rearrange("p (t d) -> p t d", t=nt),
                  src[row0:row0 + nt * 128, :].rearrange("(t p) d -> p t d", p=128), writes=[xts_t[b]])
        load(0)
        abn = 0
        on = 0
        if len(tiles) > 1:
            load(1)
        _s0, _d0, _r0, nt0, r0 = tiles[0]
        self.norm_to_hT(xts[0], xts_t[0], nt0, gs, gs_t, r0, xn, xn_t, hT, hT_t, ss, ss_t, junk, junk_t, tbanks=[0, 1])
        for ti in range(len(tiles)):
            src, dst, row0, nt, r = tiles[ti]
            b = ti % 2
            xt, xt_t = xts[b], xts_t[b]
            N = nt * 128
            for hc in range(22):
                pa = 2 + (abn % 2) * 2
                abn += 1
                bka, bta = self.banks[pa], self.bank_t[pa]
                bkb, btb = self.banks[pa + 1], self.bank_t[pa + 1]
                for (bk, bt, c0) in ((bka, bta, hc * 128), (bkb, btb, HID + hc * 128)):
                    for k in range(8):
                        lhsT = Win[:, k * 2 * HID + c0: k * 2 * HID + c0 + 128]
                        rhs = hT[:, k * N:(k + 1) * N]
                        S.op("pe", lambda E, bk=bk, lhsT=lhsT, rhs=rhs, k=k: E.matmul(bk[:, 0:N], lhsT, rhs, start=(k == 0), stop=(k == 7)),
                             reads=[Win_t, hT_t], writes=[bt])
                sb = hc % 2
                S.op("act", lambda E, bka=bka, sb=sb: E.activation(out=sa[sb][:, 0:N], in_=bka[:, 0:N], func=AF.Silu),
                     reads=[bta], writes=[sa_t[sb]])
                S.op("dve", lambda E, bkb=bkb, sb=sb, hc=hc: E.tensor_tensor(out=g[:, hc * N:(hc + 1) * N], in0=bkb[:, 0:N], in1=sa[sb][:, 0:N], op=ALU.mult),
                     reads=[btb, sa_t[sb]], writes=[g_t])
            if ti + 1 < len(tiles):
                _s1, _d1, _r1, nt1, r1 = tiles[ti + 1]
                self.norm_to_hT(xts[(ti + 1) % 2], xts_t[(ti + 1) % 2], nt1, gs, gs_t, r1, xn, xn_t, hT, hT_t, ss, ss_t, junk, junk_t,
                                tbanks=[0, 1], part="norm")
            for s in range(nt):
                for hf in range(2):
                    bi = 6 + (on % 2)
                    tb = on % 2
                    on += 1
                    bk, bt = self.banks[bi], self.bank_t[bi]
                    for hc in range(22):
                        lhsT = g[:, hc * N + s * 128: hc * N + (s + 1) * 128]
                        rhs = Wout[:, hc * D + hf * 512: hc * D + (hf + 1) * 512]
                        S.op("pe", lambda E, bk=bk, lhsT=lhsT, rhs=rhs, hc=hc: E.matmul(bk[:, :], lhsT, rhs, start=(hc == 0), stop=(hc == 21)),
                             reads=[g_t, Wout_t], writes=[bt])
                    gB = gB2[:, r * 1024 + hf * 512: r * 1024 + (hf + 1) * 512]
                    S.op("dve", lambda E, bk=bk, tb=tb, gB=gB: E.tensor_tensor(out=tmp[tb], in0=bk[:, :], in1=gB, op=ALU.mult),
                         reads=[bt, gB2_t], writes=[tmp_t[tb]])
                    xsl = xt[:, s * 1024 + hf * 512: s * 1024 + (hf + 1) * 512]
                    S.op("pool", lambda E, xsl=xsl, tb=tb: E.tensor_tensor(out=xsl, in0=tmp[tb], in1=xsl, op=ALU.add),
                         reads=[tmp_t[tb], xt_t], writes=[xt_t])
            S.dma(self.dsem(f"xst{b}"), dst[row0:row0 + nt * 128, :].rearrange("(t p) d -> p t d", p=128),
                  xt[:, 0:nt * 1024].rearrange("p (t d) -> p t d", t=nt), reads=[xt_t])
            if ti + 2 < len(tiles):
                load(ti + 2)
            if ti + 1 < len(tiles):
                _s1, _d1, _r1, nt1, r1 = tiles[ti + 1]
                self.norm_to_hT(xts[(ti + 1) % 2], xts_t[(ti + 1) % 2], nt1, gs, gs_t, r1, xn, xn_t, hT, hT_t, ss, ss_t, junk, junk_t,
                                tbanks=[0, 1], part="tr")
        S.barrier()
        A.release(m)

    def inproj_phase(self, l, src_x, src_xc):
        S, A, nc = self.S, self.A, self.nc
        T = self.T
        m = A.mark()
        dr = self.dr
        cst = self.consts
        Wm = A.alloc(8 * INW, BF16); Wm_t = Tl("Wm")
        Wm3 = Wm.rearrange("p (j n) -> p j n", j=8)
        wd = dr["mix_w_in"][l].rearrange("(j p) n -> p j n", p=128)

        def wl(dst0, src0, n):
            S.dma(self.dsem("wmix"), Wm3[:, :, dst0:dst0 + n], wd[:, :, src0:src0 + n], writes=[Wm_t], q="pool")
        wl(0, 0, 800)
        for g in range(4):
            wl(800 + (2 * g) * 64, 800 + g * 64, 64)
            wl(800 + (2 * g + 1) * 64, 800 + (4 + g) * 64, 64)
        wl(800 + 512, 1312, 128)
        for g in range(2):
            wl(800 + 640 + (2 * g) * 64, 1568 + g * 64, 64)
            wl(800 + 640 + (2 * g + 1) * 64, 1568 + (2 + g) * 64, 64)
        wl(800 + 896, 1824, 128)
        wl(1824, 1440, 128)
        wl(1952, 1952, 128)
        gs, gs_t = self.make_GS(l, "norm_mix", 2, 3)
        gains = A.alloc(256, F32); gains_t = Tl("gains")
        for i, nm in enumerate(("glb_q_norm", "glb_k_norm", "win_q_norm", "win_k_norm")):
            S.dma(self.dsem("gains"), gains[:, i * 64:(i + 1) * 64], dr[nm][l:l + 1, :].partition_broadcast(128), writes=[gains_t])
        for i in (0, 2):
            gv_ = gains[:, i * 64:(i + 1) * 64]
            S.op("dve", lambda E, gv_=gv_: E.tensor_scalar(out=gv_, in0=gv_, scalar1=0.125, scalar2=None, op0=ALU.mult),
                 reads=[gains_t], writes=[gains_t])
        Wg = A.alloc(256, F32); Wg_t = Tl("Wg")
        S.op("dve", lambda E: E.memset(Wg[0:32, :], 0.0), writes=[Wg_t])
        S.dma(self.dsem("wg"), Wg[0:16, 0:128], dr["gla_wg_f"][l], writes=[Wg_t])
        S.dma(self.dsem("wg"), Wg[16:32, 128:256], dr["gla_wg_b"][l], writes=[Wg_t])
        bgr = A.alloc(256, F32); bgr_t = Tl("bgr")
        S.dma(self.dsem("bg"), bgr[0:1, 0:128], dr["gla_bg_f"][l:l + 1, :], writes=[bgr_t])
        S.dma(self.dsem("bg"), bgr[0:1, 128:256], dr["gla_bg_b"][l:l + 1, :], writes=[bgr_t])
        NT = 2
        xts = [A.alloc(NT * 1024, F32) for _ in range(2)]
        xts_t = [Tl(f"ixt{i}") for i in range(2)]
        xn = A.alloc(NT * 1024, BF16); xn_t = Tl("xn")
        hT = A.alloc(8 * NT * 128, BF16); hT_t = Tl("hT")
        ss = A.alloc(16, F32); ss_t = Tl("ss")
        junk = A.alloc(1024, BF16); junk_t = Tl("junk")
        adT = A.alloc(256, F32); adT_t = Tl("adT")
        gst = [A.alloc(1024, F32) for _ in range(2)]; gst_t = [Tl("gst0"), Tl("gst1")]
        vst = [A.alloc(512, BF16) for _ in range(2)]; vst_t = [Tl("vst0"), Tl("vst1")]
        for i in range(2):
            S.op("pool", lambda E, i=i: E.memset(vst[i], 1.0), writes=[vst_t[i]])
        sq = A.alloc(1024, F32); sq_t = Tl("sq")
        st = A.alloc(64, F32); st_t = Tl("st")
        qk = A.alloc(1024, F32); qk_t = Tl("qk")
        rt = [A.alloc(512, F32) for _ in range(4)]; rt_t = [Tl(f"rt{i}") for i in range(4)]
        qkb = A.alloc(1024, BF16); qkb_t = Tl("qkb")
        qkT = [A.alloc(8 * 256, BF16) for _ in range(2)]; qkT_t = [Tl("qkT0"), Tl("qkT1")]
        cs = [A.alloc(64, F32) for _ in range(2)]; cs_t = [Tl("cs0"), Tl("cs1")]
        eg = A.alloc(256, F32); eg_t = Tl("eg")
        B, BT = self.banks, self.bank_t
        tiles = [(src_xc, 0, 0, 2, 1, None)]
        for i in range(T // 256):
            tiles.append((src_x, i * 256, CTX + i * 256, 2, 0, i * 256))

        def load(ti):
            src, row0, tok0, nt, r, lat0 = tiles[ti]
            b = ti % 2
            S.dma(self.dsem(f"ixld{b}"), xts[b][:, 0:nt * 1024].rearrange("p (t d) -> p t d", t=nt),
                  src[row0:row0 + nt * 128, :].rearrange("(t p) d -> p t d", p=128), writes=[xts_t[b]])
        load(0)
        nsub = 0
        for ti in range(len(tiles)):
            src, row0, tok0, nt, r, lat0 = tiles[ti]
            if ti + 1 < len(tiles):
                load(ti + 1)
            b = ti % 2
            N = nt * 128
            self.norm_to_hT(xts[b], xts_t[b], nt, gs, gs_t, r, xn, xn_t, hT, hT_t, ss, ss_t, junk, junk_t, tbanks=[0, 1])
            for k in range(8):
                S.op("pe", lambda E, k=k: E.matmul(B[6][0:32, 256:256 + N], Wm3[:, k, 768:800], hT[:, k * N:(k + 1) * N], start=(k == 0), stop=(k == 7)),
                     reads=[Wm_t, hT_t], writes=[BT[6]])
            S.op("act", lambda E: E.activation(out=adT[0:32, 0:N], in_=B[6][0:32, 256:256 + N], func=AF.Copy), reads=[BT[6]], writes=[adT_t])
            qT = qkT[ti % 2]; qT_t = qkT_t[ti % 2]
            for s_ in range(nt):
                sb = nsub % 2
                nsub += 1
                tk = tok0 + s_ * 128
                if lat0 is not None:
                    S.dma(self.dsem(f"cs{sb}"), cs[sb], dr["rope"][lat0 + s_ * 128: lat0 + (s_ + 1) * 128, :], writes=[cs_t[sb]])
                for (bi, c0, n) in ((2, 0, 512), (3, 512, 256), (4, 800, 512), (5, 1312, 512), (6, 1824, 256)):
                    for k in range(8):
                        S.op("pe", lambda E, bi=bi, c0=c0, n=n, k=k, s_=s_: E.matmul(B[bi][:, 0:n], hT[:, k * N + s_ * 128: k * N + (s_ + 1) * 128],
                                                                                     Wm3[:, k, c0:c0 + n], start=(k == 0), stop=(k == 7)),
                             reads=[Wm_t, hT_t], writes=[BT[bi]])
                S.op("pe", lambda E, s_=s_: E.matmul(B[3][:, 256:512], adT[0:32, s_ * 128:(s_ + 1) * 128], Wg[0:32, :], start=True, stop=False),
                     reads=[adT_t, Wg_t], writes=[BT[3]])
                S.op("pe", lambda E: E.matmul(B[3][:, 256:512], self.ones_f[0:1, :], bgr[0:1, :], start=False, stop=True),
                     reads=[self.ones_t, bgr_t], writes=[BT[3]])
                g_ = gst[sb]; g_t = gst_t[sb]
                S.op("act", lambda E, g_=g_: E.activation(out=g_[:, 0:512], in_=B[2][:, :], func=AF.Copy), reads=[BT[2]], writes=[g_t])
                S.op("dve", lambda E, g_=g_: E.tensor_copy(out=g_[:, 512:768], in_=B[3][:, 0:256]), reads=[BT[3]], writes=[g_t])
                S.op("act", lambda E: E.activation(out=eg, in_=B[3][:, 256:512], func=AF.Exp, scale=-1.0), reads=[BT[3]], writes=[eg_t])
                S.op("act", lambda E: E.activation(out=eg, in_=eg, func=AF.Ln, bias=1.0), reads=[eg_t], writes=[eg_t])
                S.op("pool", lambda E, g_=g_: E.tensor_scalar(out=g_[:, 768:1024], in0=eg, scalar1=-1.0 / 16.0, scalar2=None, op0=ALU.mult),
                     reads=[eg_t], writes=[g_t])
                S.dma(self.dsem(f"gstst{sb}"), dr["gla_d"][tk:tk + 128, :], g_, reads=[g_t])
                v_ = vst[sb]; v_t = vst_t[sb]
                S.op("dve", lambda E, v_=v_: E.tensor_copy(out=v_.rearrange("p (a c) -> p a c", a=4)[:, :, 0:64],
                                                          in_=B[6][:, 0:256].rearrange("p (a c) -> p a c", a=4)),
                     reads=[BT[6]], writes=[v_t])
                S.dma(self.dsem(f"vstst{sb}"), dr["v_d"][tk:tk + 128, :], v_, reads=[v_t])
                S.op("act", lambda E: E.activation(out=sq[:, 0:512], in_=B[4][:, :], func=AF.Square), reads=[BT[4]], writes=[sq_t])
                S.op("act", lambda E: E.activation(out=sq[:, 512:1024], in_=B[5][:, :], func=AF.Square), reads=[BT[5]], writes=[sq_t])
                S.op("dve", lambda E: E.tensor_reduce(out=st[:, 0:16], in_=sq.rearrange("p (h d) -> p h d", d=64), axis=AX.X, op=ALU.add),
                     reads=[sq_t], writes=[st_t])
                S.op("dve", lambda E: E.tensor_scalar(out=st[:, 16:32], in0=st[:, 0:16], scalar1=1.0 / 64, scalar2=EPS, op0=ALU.mult, op1=ALU.add),
                     reads=[st_t], writes=[st_t])
                S.op("act", lambda E: E.activation(out=st[:, 32:48], in_=st[:, 16:32], func=AF.Sqrt), reads=[st_t], writes=[st_t])
                S.op("dve", lambda E: E.reciprocal(out=st[:, 48:64], in_=st[:, 32:48]), reads=[st_t], writes=[st_t])
                for hb, bi in ((0, 4), (1, 5)):
                    S.op("dve", lambda E, hb=hb, bi=bi: E.tensor_tensor(
                        out=qk[:, hb * 512:(hb + 1) * 512].rearrange("p (h d) -> p h d", d=64),
                        in0=B[bi][:, :].rearrange("p (h d) -> p h d", d=64),
                        in1=st[:, 48 + hb * 8: 56 + hb * 8].unsqueeze(2).to_broadcast([128, 8, 64]), op=ALU.mult),
                        reads=[BT[bi], st_t], writes=[qk_t])
                qk3 = qk.rearrange("p (h d) -> p h d", d=64)
                for (h0, h1, gi_) in ((0, 8, 0), (8, 10, 1), (10, 14, 2), (14, 16, 3)):
                    S.op("pool", lambda E, h0=h0, h1=h1, gi_=gi_: E.tensor_tensor(
                        out=qk3[:, h0:h1, :], in0=qk3[:, h0:h1, :],
                        in1=gains[:, gi_ * 64:(gi_ + 1) * 64].unsqueeze(1).to_broadcast([128, h1 - h0, 64]), op=ALU.mult),
                        reads=[qk_t, gains_t], writes=[qk_t])
                qkb3 = qkb.rearrange("p (h d) -> p h d", d=64)
                if lat0 is not None:
                    cosb = cs[sb][:, 0:32].unsqueeze(1).to_broadcast([128, 16, 32])
                    sinb = cs[sb][:, 32:64].unsqueeze(1).to_broadcast([128, 16, 32])
                    x1 = qk3[:, :, 0:32]; x2 = qk3[:, :, 32:64]
                    r3 = [t_.rearrange("p (h d) -> p h d", d=32) for t_ in rt]
                    S.op("dve", lambda E, cosb=cosb: E.tensor_tensor(out=r3[0], in0=x1, in1=cosb, op=ALU.mult), reads=[qk_t, cs_t[sb]], writes=[rt_t[0]])
                    S.op("pool", lambda E, sinb=sinb: E.tensor_tensor(out=r3[1], in0=x2, in1=sinb, op=ALU.mult), reads=[qk_t, cs_t[sb]], writes=[rt_t[1]])
                    S.op("dve", lambda E: E.tensor_tensor(out=qkb3[:, :, 0:32], in0=r3[0], in1=r3[1], op=ALU.subtract),
                         reads=[rt_t[0], rt_t[1]], writes=[qkb_t])
                    S.op("pool", lambda E, sinb=sinb: E.tensor_tensor(out=r3[2], in0=x1, in1=sinb, op=ALU.mult), reads=[qk_t, cs_t[sb]], writes=[rt_t[2]])
                    S.op("dve", lambda E, cosb=cosb: E.tensor_tensor(out=r3[3], in0=x2, in1=cosb, op=ALU.mult), reads=[qk_t, cs_t[sb]], writes=[rt_t[3]])
                    S.op("pool", lambda E: E.tensor_tensor(out=qkb3[:, :, 32:64], in0=r3[2], in1=r3[3], op=ALU.add),
                         reads=[rt_t[2], rt_t[3]], writes=[qkb_t])
                else:
                    S.op("act", lambda E: E.activation(out=qkb, in_=qk, func=AF.Copy), reads=[qk_t], writes=[qkb_t])
                for half in range(2):
                    bi = half
                    bkT = B[bi][:, :].bitcast(BF16)
                    for c in range(4):
                        cc = half * 4 + c
                        S.op("pe", lambda E, bkT=bkT, c=c, cc=cc: E.transpose(out=bkT[:, c * 128:(c + 1) * 128], in_=qkb[:, cc * 128:(cc + 1) * 128],
                                                                          identity=self.ident),
                             reads=[qkb_t, self.ident_t], writes=[BT[bi]])
                    dstv = qT.rearrange("p (c n) -> p c n", c=8)[:, half * 4:(half + 1) * 4, s_ * 128:(s_ + 1) * 128]
                    srcv = bkT[:, 0:512].rearrange("p (c n) -> p c n", c=4)
                    if half == 0:
                        S.op("act", lambda E, dstv=dstv, srcv=srcv: E.activation(out=dstv, in_=srcv, func=AF.Copy), reads=[BT[bi]], writes=[qT_t])
                    else:
                        S.op("dve", lambda E, dstv=dstv, srcv=srcv: E.tensor_copy(out=dstv, in_=srcv), reads=[BT[bi]], writes=[qT_t])
            S.dma(self.dsem(f"qkTst{ti % 2}"), dr["qkT_d"][:, :, tok0:tok0 + N], qT.rearrange("p (c n) -> p c n", c=8)[:, :, 0:N], reads=[qT_t])
        S.barrier()
        A.release(m)

    def _attn_jobs(self, jobs, KT, KT_t, V, V_t, lookahead=3):
        S = self.S
        B, BT = self.banks, self.bank_t
        PT, PT_t = self._PT, self._PT_t
        st = self._attn_state
        for job in jobs:
            nq = job["nq"]
            pair = (nq == 512)
            steps = []
            for kv in range(2):
                tl = job["tiles"]
                if pair:
                    assert len(tl) % 2 == 0
                    for i in range(0, len(tl), 2):
                        steps.append((kv, [tl[i], tl[i + 1]], i == 0, i + 2 >= len(tl)))
                else:
                    for i in range(len(tl)):
                        steps.append((kv, [tl[i]], i == 0, i == len(tl) - 1))
            slots = []
            obank = {}

            def issue_qk(j, job=job, nq=nq, steps=steps, slots=slots):
                kv, tls, first, last = steps[j]
                sp = st["s"] % 3
                st["s"] += 1
                rhs, rhs_t = job["rhs"](kv)
                big = self.bigs[sp]
                bts = [BT[2 * sp], BT[2 * sp + 1]]
                for u, (kt, mk) in enumerate(tls):
                    S.op("pe", lambda E, big=big, u=u, kv=kv, kt=kt, rhs=rhs, nq=nq: E.matmul(
                        big[:, u * 512:u * 512 + nq], KT[kv * 64:(kv + 1) * 64, kt * 128:(kt + 1) * 128], rhs, start=True, stop=True),
                        reads=[KT_t, rhs_t], writes=[bts[u]])
                w = (len(tls) - 1) * 512 + nq
                S.op("act", lambda E, sp=sp, big=big, w=w: E.activation(out=PT[sp][:, 0:w], in_=big[:, 0:w], func=AF.Exp),
                     reads=bts[:len(tls)], writes=[PT_t[sp]])
                for u, (kt, mk) in enumerate(tls):
                    if mk is not None:
                        pv_ = PT[sp][:, u * 512:u * 512 + nq].rearrange("p (g n) -> p g n", n=128)
                        S.op("pool", lambda E, pv_=pv_, mk=mk, nq=nq: E.tensor_tensor(out=pv_, in0=pv_, in1=mk.unsqueeze(1).to_broadcast([128, nq // 128, 128]),
                                                                                      op=ALU.mult),
                             reads=[PT_t[sp], self.consts_t], writes=[PT_t[sp]])
                slots.append(sp)

            def issue_pv(j, job=job, nq=nq, steps=steps, slots=slots, obank=obank):
                kv, tls, first, last = steps[j]
                if first:
                    obank[kv] = 6 + (st["o"] % 2)
                    st["o"] += 1
                ob = obank[kv]
                sp = slots[j]
                for u, (kt, mk) in enumerate(tls):
                    f_ = first and u == 0
                    l_ = last and u == len(tls) - 1
                    S.op("pe", lambda E, ob=ob, kv=kv, kt=kt, sp=sp, u=u, f_=f_, l_=l_, nq=nq: E.matmul(
                        B[ob][:, 0:nq], V[:, kt * 256 + kv * 128: kt * 256 + (kv + 1) * 128], PT[sp][:, u * 512:u * 512 + nq], start=f_, stop=l_),
                        reads=[V_t, PT_t[sp]], writes=[BT[ob]])
                if last:
                    job["fin"](kv, ob)
            n = len(steps)
            la = min(lookahead - 1, n)
            for j in range(la):
                issue_qk(j)
            for j in range(n):
                if j + la < n:
                    issue_qk(j + la)
                issue_pv(j)
            if "end" in job:
                job["end"]()

    def _load_kv(self, l, kchunk, vcol0):
        S, A, dr = self.S, self.A, self.dr
        TC = self.T + CTX
        NKT = TC // 128
        KT = A.alloc(TC, BF16); KT_t = Tl("KT")
        S.dma(self.dsem("ktld"), KT, dr["qkT_d"][:, kchunk, :], writes=[KT_t])
        V = A.alloc(NKT * 256, BF16); V_t = Tl("V")
        V3 = V.rearrange("p (k c) -> p k c", c=256)
        vd = dr["v_d"][:, vcol0:vcol0 + 256].rearrange("(k p) c -> p k c", p=128)
        for k0 in range(0, NKT, 11):
            k1 = min(NKT, k0 + 11)
            S.dma(self.dsem("vld"), V3[:, k0:k1, :], vd[:, k0:k1, :], writes=[V_t])
        self._PT = [A.alloc(1024, BF16) for _ in range(3)]
        self._PT_t = [Tl(f"PT{i}") for i in range(3)]
        self._attn_state = {"s": 0, "o": 0}
        return KT, KT_t, V, V_t, NKT

    def gattn_phase(self, l, need_ctx):
        S, A, dr = self.S, self.A, self.dr
        T = self.T
        m = A.mark()
        KT, KT_t, V, V_t, NKT = self._load_kv(l, 4, 0)
        QN = min(512, T)
        QT = [A.alloc(512, BF16) for _ in range(2)]; QT_t = [Tl("QT0"), Tl("QT1")]
        rden = [A.alloc(512, F32) for _ in range(2)]; rden_t = [Tl("rden0"), Tl("rden1")]
        OT = [A.alloc(4 * 512, BF16) for _ in range(2)]; OT_t = [Tl("OT0"), Tl("OT1")]
        B, BT = self.banks, self.bank_t
        qjobs = []
        if need_ctx:
            qjobs.append((0, CTX, [(0, None), (1, None)]))
        for qt in range(T // QN):
            qjobs.append((CTX + qt * QN, QN, [(k, None) for k in range(NKT)]))
        jobs = []
        cnt = [0, 0]
        for qi, (q0, nq, tl) in enumerate(qjobs):
            ot = OT[qi % 2]; ot_t = OT_t[qi % 2]
            for g in range(4):
                qb = (qi * 4 + g) % 2
                job = {"nq": nq, "tiles": tl}

                def pre(q0=q0, nq=nq, g=g, qb=qb):
                    S.dma(self.dsem(f"qtld{qb}"), QT[qb][:, 0:nq], dr["qkT_d"][:, g, q0:q0 + nq], writes=[QT_t[qb]])
                job["pre"] = pre
                job["rhs"] = (lambda kv, qb=qb, nq=nq: (QT[qb][kv * 64:(kv + 1) * 64, 0:nq], QT_t[qb]))

                def fin(kv, ob, g=g, nq=nq, ot=ot, ot_t=ot_t):
                    rb = cnt[0] % 2
                    cnt[0] += 1
                    S.op("dve", lambda E, rb=rb, ob=ob: E.reciprocal(out=rden[rb][64:128, 0:nq], in_=B[ob][64:128, 0:nq]), reads=[BT[ob]], writes=[rden_t[rb]])
                    S.op("dve", lambda E, rb=rb, ob=ob, kv=kv: E.tensor_tensor(out=ot[kv * 64:(kv + 1) * 64, g * nq:(g + 1) * nq], in0=B[ob][0:64, 0:nq],
                                                                               in1=rden[rb][64:128, 0:nq], op=ALU.mult),
                         reads=[BT[ob], rden_t[rb]], writes=[ot_t])
                job["fin"] = fin
                if g == 3:
                    def end(q0=q0, nq=nq, ot=ot, ot_t=ot_t, qi=qi):
                        S.dma(self.dsem(f"otst{qi % 2}"), dr["mixT_d"][:, 2:6, q0:q0 + nq], ot[:, 0:4 * nq].rearrange("p (c n) -> p c n", c=4), reads=[ot_t])
                    job["end"] = end
                jobs.append(job)
        if jobs:
            jobs[0]["pre"]()
        for j, job in enumerate(jobs):
            if j + 1 < len(jobs):
                jobs[j + 1]["pre"]()
            self._attn_jobs([job], KT, KT_t, V, V_t)
        S.barrier()
        A.release(m)

    def wattn_phase(self, l, need_ctx):
        S, A, dr = self.S, self.A, self.dr
        T = self.T
        TC = T + CTX
        m = A.mark()
        KT, KT_t, V, V_t, NKT = self._load_kv(l, 7, 256)
        QW = A.alloc(2 * TC, BF16); QW_t = Tl("QW")
        QW3 = QW.rearrange("p (c n) -> p c n", c=2)
        S.dma(self.dsem("qwld"), QW3, dr["qkT_d"][:, 5:7, :], writes=[QW_t])
        esink = A.alloc(8, F32); esink_t = Tl("esink")
        S.dma(self.dsem("sink"), esink[:, 0:4], dr["win_sink"][l:l + 1, :].partition_broadcast(128), writes=[esink_t])
        S.op("act", lambda E: E.activation(out=esink[:, 4:8], in_=esink[:, 0:4], func=AF.Exp), reads=[esink_t], writes=[esink_t])
        rd = [A.alloc(256, F32) for _ in range(2)]; rd_t = [Tl("rd0"), Tl("rd1")]
        OT = [A.alloc(256, BF16) for _ in range(2)]; OT_t = [Tl("wOT0"), Tl("wOT1")]
        B, BT = self.banks, self.bank_t
        mprev = self.consts[:, C_WM:C_WM + 128]
        mnext = self.consts[:, C_WM + 128:C_WM + 256]
        blocks = []
        if need_ctx:
            for i in range(2):
                blocks.append((i * 128, [(0, None), (1, None)]))
        nb = T // 128
        for i in range(nb):
            tl = []
            if i > 0:
                tl.append((2 + i - 1, mprev))
            tl.append((2 + i, None))
            if i < nb - 1:
                tl.append((2 + i + 1, mnext))
            tl += [(0, None), (1, None)]
            blocks.append((CTX + i * 128, tl))
        cnt = [0]
        jobs = []
        for bi_, (q0, tl) in enumerate(blocks):
            ot = OT[bi_ % 2]; ot_t = OT_t[bi_ % 2]
            job = {"nq": 256, "tiles": tl}
            job["rhs"] = (lambda kv, q0=q0: (QW3[kv * 64:(kv + 1) * 64, :, q0:q0 + 128], QW_t))

            def fin(kv, ob, ot=ot, ot_t=ot_t):
                rb = cnt[0] % 2
                cnt[0] += 1
                for g in range(2):
                    h = kv * 2 + g
                    S.op("dve", lambda E, rb=rb, ob=ob, g=g, h=h: E.tensor_scalar(out=rd[rb][64:128, g * 128:(g + 1) * 128], in0=B[ob][64:128, g * 128:(g + 1) * 128],
                                                                                 scalar1=esink[64:128, 4 + h:5 + h], scalar2=None, op0=ALU.add),
                         reads=[BT[ob], esink_t], writes=[rd_t[rb]])
                S.op("dve", lambda E, rb=rb: E.reciprocal(out=rd[rb][64:128, :], in_=rd[rb][64:128, :]), reads=[rd_t[rb]], writes=[rd_t[rb]])
                S.op("dve", lambda E, rb=rb, ob=ob, kv=kv: E.tensor_tensor(out=ot[kv * 64:(kv + 1) * 64, :], in0=B[ob][0:64, 0:256], in1=rd[rb][64:128, :], op=ALU.mult),
                     reads=[BT[ob], rd_t[rb]], writes=[ot_t])
            job["fin"] = fin

            def end(q0=q0, ot=ot, ot_t=ot_t, bi_=bi_):
                S.dma(self.dsem(f"wotst{bi_ % 2}"), dr["mixT_d"][:, 6:8, q0:q0 + 128], ot.rearrange("p (c n) -> p c n", c=2), reads=[ot_t])
            job["end"] = end
            jobs.append(job)
        self._attn_jobs(jobs, KT, KT_t, V, V_t)
        S.barrier()
        A.release(m)

    def gla_phase(self, l, need_ctx):
        S, A, dr = self.S, self.A, self.dr
        T = self.T
        TC = T + CTX
        NCH = TC // 64
        m = A.mark()
        cst = self.consts; cst_t = self.consts_t
        B = self.banks
        Q = slice(0, 64)

        def two(n, dt=F32, nm=""):
            return [A.alloc(n, dt) for _ in range(2)], [Tl(nm + "0"), Tl(nm + "1")]
        raw, raw_t = two(1024, F32, "raw")
        ofl, ofl_t = two(256, F32, "ofl")
        ofs_, ofs_t = two(256, F32, "ofs")
        Sst = A.alloc(256, F32); Sst_t = Tl("Sst")
        Sbf, Sbf_t = two(256, BF16, "Sbf")
        gain_o = A.alloc(64, F32); gain_o_t = Tl("gain_o")
        S.dma(self.dsem("gaino"), gain_o, dr["gla_out_norm"][l:l + 1, :].partition_broadcast(128), writes=[gain_o_t])
        ex = [[A.alloc(128, F32) for _ in range(3)] for _ in range(2)]
        ex_t = [[Tl(f"ex{p}{i}") for i in range(3)] for p in range(2)]
        dec, dec_t = two(2, F32, "dec")
        qin, qin_t = two(128, BF16, "qin")
        kout, kout_t = two(128, BF16, "kout")
        k2, k2_t = two(128, BF16, "k2")
        vb, vb_t = two(256, BF16, "vb")
        koutT, koutT_t = two(64, BF16, "koutT")
        qinT, qinT_t = two(64, BF16, "qinT")
        Qbd, Qbd_t = two(256, BF16, "Qbd")
        ATm, ATm_t = two(256, BF16, "ATm")
        dsm, dsm_t = two(256, F32, "dsm")
        ob_ = A.alloc(256, F32); ob_t = Tl("ob")
        sqo = A.alloc(256, F32); sqo_t = Tl("sqo")
        s4 = A.alloc(16, F32); s4_t = Tl("s4")
        sg = A.alloc(256, F32); sg_t = Tl("sg")
        ofin = A.alloc(256, BF16); ofin_t = Tl("ofin")
        oT, oT_t = two(128, BF16, "oT")
        for bf_, bf_t in ((raw[0], raw_t[0]), (raw[1], raw_t[1]), (qin[0], qin_t[0]), (qin[1], qin_t[1]),
                          (kout[0], kout_t[0]), (kout[1], kout_t[1]), (ofin, ofin_t)):
            S.op("pool", lambda E, bf_=bf_: E.memset(bf_, 0.0), writes=[bf_t])
        TRI = [cst[:, C_TRI + i * 128: C_TRI + (i + 1) * 128] for i in range(4)]
        GM = [cst[Q, C_GM + i * 64: C_GM + (i + 1) * 64] for i in range(2)]
        BD = cst[:, C_BD:C_BD + 4]
        idt = self.ident
        BT = self.bank_t
        Xcum_t = [BT[0], BT[1]]; Xgam_t = Xcum_t; Xtr_t = Xcum_t
        YA_t = [BT[2], BT[3]]; Yds_t = [BT[6], BT[6]]
        O_t = [BT[4], BT[5]]; b4_t = BT[7]
        bX = [B[p][:, :].bitcast(BF16) for p in range(2)]
        ofd = dr["of_d"]
        nout = [0]
        for d in range(2):
            S.op("dve", lambda E: E.memset(Sst, 0.0), writes=[Sst_t])
            for p in range(2):
                S.op("pool", lambda E, p=p: E.memset(Sbf[p], 0.0), writes=[Sbf_t[p]])
            order = list(range(NCH)) if d == 0 else [3, 2, 1, 0] + list(range(NCH - 1, 3, -1))
            n_ = len(order)

            def load(i, d=d, order=order):
                ch = order[i]
                S.dma(self.dsem(f"rawld{i % 2}"), raw[i % 2][Q, :], dr["gla_d"][ch * 64:(ch + 1) * 64, :], writes=[raw_t[i % 2]])
                if d == 1 and (ch >= 4 or need_ctx):
                    S.dma(self.dsem(f"ofld{i % 2}"), ofl[i % 2][Q, :], ofd[ch * 64:(ch + 1) * 64, :], writes=[ofl_t[i % 2]])

            def prep(i, d=d):
                p = i % 2
                rw = raw[p]; rw_t = raw_t[p]
                X = B[p]; Y = B[2 + p]
                g = rw[:, 768 + d * 128: 768 + (d + 1) * 128]
                S.op("pe", lambda E: E.matmul(X[:, 0:128], TRI[2 * d], g, start=True, stop=True), reads=[cst_t, rw_t], writes=[Xcum_t[p]])
                S.op("pe", lambda E: E.matmul(X[:, 128:256], TRI[2 * d + 1], g, start=True, stop=True), reads=[cst_t, rw_t], writes=[Xcum_t[p]])
                S.op("pe", lambda E: E.matmul(X[:, 256:258], g, self.ones_f[:, 0:2], start=True, stop=True), reads=[self.ones_t, rw_t], writes=[Xgam_t[p]])
                e0, e1, e2 = ex[p]
                S.op("act", lambda E: E.activation(out=e0[Q, :], in_=X[Q, 0:128], func=AF.Exp), reads=[Xcum_t[p]], writes=[ex_t[p][0]])
                S.op("act", lambda E: E.activation(out=e1[Q, :], in_=X[Q, 0:128], func=AF.Exp, scale=-1.0), reads=[Xcum_t[p]], writes=[ex_t[p][1]])
                S.op("act", lambda E: E.activation(out=e2[Q, :], in_=X[Q, 128:256], func=AF.Exp), reads=[Xcum_t[p]], writes=[ex_t[p][2]])
                S.op("act", lambda E: E.activation(out=dec[p], in_=X[:, 256:258], func=AF.Exp), reads=[Xgam_t[p]], writes=[dec_t[p]])
                S.op("dve", lambda E: E.scalar_tensor_tensor(out=qin[p][Q, :], in0=rw[Q, 0:128], scalar=32.0 ** -0.5, in1=e0[Q, :], op0=ALU.mult, op1=ALU.mult),
                     reads=[rw_t, ex_t[p][0]], writes=[qin_t[p]])
                S.op("pool", lambda E: E.tensor_tensor(out=kout[p][Q, :], in0=rw[Q, 128:256], in1=e1[Q, :], op=ALU.mult), reads=[rw_t, ex_t[p][1]], writes=[kout_t[p]])
                S.op("pool", lambda E: E.tensor_tensor(out=k2[p][Q, :], in0=rw[Q, 128:256], in1=e2[Q, :], op=ALU.mult), reads=[rw_t, ex_t[p][2]], writes=[k2_t[p]])
                S.op("act", lambda E: E.activation(out=vb[p][Q, :], in_=rw[Q, 256:512], func=AF.Copy), reads=[rw_t], writes=[vb_t[p]])
                S.op("pe", lambda E: E.transpose(out=bX[p][:, 768:896], in_=qin[p], identity=idt), reads=[qin_t[p], self.ident_t], writes=[Xtr_t[p]])
                S.op("pe", lambda E: E.transpose(out=bX[p][:, 896:1024], in_=kout[p], identity=idt), reads=[kout_t[p], self.ident_t], writes=[Xtr_t[p]])
                S.op("act", lambda E: E.activation(out=koutT[p], in_=bX[p][:, 896:960], func=AF.Copy), reads=[Xtr_t[p]], writes=[koutT_t[p]])
                S.op("act", lambda E: E.activation(out=qinT[p], in_=bX[p][:, 768:832], func=AF.Copy), reads=[Xtr_t[p]], writes=[qinT_t[p]])
                for h in range(4):
                    eng_ = "dve" if h % 2 == 0 else "pool"
                    S.op(eng_, lambda E, h=h: E.tensor_scalar(out=Qbd[p][:, h * 64:(h + 1) * 64], in0=qinT[p], scalar1=BD[:, h:h + 1], scalar2=None, op0=ALU.mult),
                         reads=[qinT_t[p], cst_t], writes=[Qbd_t[p]])
                S.op("pe", lambda E: E.matmul(Y[Q, 0:256], koutT[p], Qbd[p], start=True, stop=True), reads=[koutT_t[p], Qbd_t[p]], writes=[YA_t[p]])
                S.op("dve", lambda E: E.tensor_tensor(out=ATm[p][Q, :].rearrange("p (h t) -> p h t", h=4),
                                                     in0=Y[Q, 0:256].rearrange("p (h t) -> p h t", h=4),
                                                     in1=GM[d].unsqueeze(1).to_broadcast([64, 4, 64]), op=ALU.mult),
                     reads=[YA_t[p], cst_t], writes=[ATm_t[p]])
                S.op("pe", lambda E: E.matmul(B[6][:, 0:256], k2[p][Q, :], vb[p][Q, :], start=True, stop=True), reads=[k2_t[p], vb_t[p]], writes=[Yds_t[p]])
                S.op("dve", lambda E: E.tensor_tensor(out=dsm[p].rearrange("p (h v) -> p h v", h=4), in0=B[6][:, 0:256].rearrange("p (h v) -> p h v", h=4),
                                                     in1=BD.unsqueeze(2).to_broadcast([128, 4, 64]), op=ALU.mult),
                     reads=[Yds_t[p], cst_t], writes=[dsm_t[p]])

            def scan(i, d=d, order=order):
                p = i % 2
                ch = order[i]
                rw = raw[p]; rw_t = raw_t[p]
                O = B[4 + p]
                sprev = Sbf[(i + 1) % 2]; sprev_t = Sbf_t[(i + 1) % 2]
                for h in range(4):
                    S.op("pe", lambda E, h=h: E.matmul(O[Q, h * 64:(h + 1) * 64], qinT[p], sprev[:, h * 64:(h + 1) * 64], start=True, stop=False),
                         reads=[qinT_t[p], sprev_t], writes=[O_t[p]])
                    S.op("pe", lambda E, h=h: E.matmul(O[Q, h * 64:(h + 1) * 64], ATm[p][Q, h * 64:(h + 1) * 64], vb[p][Q, h * 64:(h + 1) * 64],
                                                       start=False, stop=True),
                         reads=[ATm_t[p], vb_t[p]], writes=[O_t[p]])
                S.op("dve", lambda E: E.scalar_tensor_tensor(out=Sst, in0=Sst, scalar=dec[p][:, 0:1], in1=dsm[p], op0=ALU.mult, op1=ALU.add),
                     reads=[Sst_t, dec_t[p], dsm_t[p]], writes=[Sst_t])
                S.op("act", lambda E: E.activation(out=Sbf[p], in_=Sst, func=AF.Copy), reads=[Sst_t], writes=[Sbf_t[p]])
                if d == 0:
                    os_ = ofs_[p]; os_t = ofs_t[p]
                    S.op("act", lambda E: E.activation(out=os_[Q, :], in_=O[Q, 0:256], func=AF.Copy), reads=[O_t[p]], writes=[os_t])
                    S.dma(self.dsem(f"ofst{p}"), ofd[ch * 64:(ch + 1) * 64, :], os_[Q, :], reads=[os_t])
                elif ch >= 4 or need_ctx:
                    ol = ofl[p]; ol_t = ofl_t[p]
                    S.op("dve", lambda E: E.tensor_tensor(out=ob_[Q, :], in0=O[Q, 0:256], in1=ol[Q, :], op=ALU.add), reads=[O_t[p], ol_t], writes=[ob_t])
                    S.op("act", lambda E: E.activation(out=sqo[Q, :], in_=ob_[Q, :], func=AF.Square), reads=[ob_t], writes=[sqo_t])
                    S.op("dve", lambda E: E.tensor_reduce(out=s4[Q, 0:4], in_=sqo[Q, :].rearrange("p (h v) -> p h v", h=4), axis=AX.X, op=ALU.add),
                         reads=[sqo_t], writes=[s4_t])
                    S.op("dve", lambda E: E.tensor_scalar(out=s4[Q, 4:8], in0=s4[Q, 0:4], scalar1=1.0 / 64, scalar2=EPS, op0=ALU.mult, op1=ALU.add),
                         reads=[s4_t], writes=[s4_t])
                    S.op("act", lambda E: E.activation(out=s4[Q, 8:12], in_=s4[Q, 4:8], func=AF.Sqrt), reads=[s4_t], writes=[s4_t])
                    S.op("dve", lambda E: E.reciprocal(out=s4[Q, 12:16], in_=s4[Q, 8:12]), reads=[s4_t], writes=[s4_t])
                    ob3 = ob_[Q, :].rearrange("p (h v) -> p h v", h=4)
                    S.op("dve", lambda E: E.tensor_tensor(out=ob3, in0=ob3, in1=s4[Q, 12:16].unsqueeze(2).to_broadcast([64, 4, 64]), op=ALU.mult),
                         reads=[ob_t, s4_t], writes=[ob_t])
                    S.op("pool", lambda E: E.tensor_tensor(out=ob3, in0=ob3, in1=gain_o[Q, :].unsqueeze(1).to_broadcast([64, 4, 64]), op=ALU.mult),
                         reads=[ob_t, gain_o_t], writes=[ob_t])
                    S.op("act", lambda E: E.activation(out=sg[Q, :], in_=rw[Q, 512:768], func=AF.Silu), reads=[rw_t], writes=[sg_t])
                    S.op("dve", lambda E: E.tensor_tensor(out=ofin[Q, :], in0=ob_[Q, :], in1=sg[Q, :], op=ALU.mult), reads=[ob_t, sg_t], writes=[ofin_t])
                    b4 = B[7][:, :].bitcast(BF16)
                    for c in range(2):
                        S.op("pe", lambda E, c=c: E.transpose(out=b4[:, c * 128:(c + 1) * 128], in_=ofin[:, c * 128:(c + 1) * 128], identity=idt),
                             reads=[ofin_t, self.ident_t], writes=[b4_t])
                    q_ = nout[0] % 2
                    ot = oT[q_]; ot_t = oT_t[q_]
                    S.op("act", lambda E: E.activation(out=ot.rearrange("p (c n) -> p c n", c=2),
                                                       in_=b4[:, 0:256].rearrange("p (c n) -> p c n", c=2)[:, :, 0:64], func=AF.Copy),
                         reads=[b4_t], writes=[ot_t])
                    S.dma(self.dsem(f"glaost{q_}"), dr["mixT_d"][:, 0:2, ch * 64:(ch + 1) * 64], ot.rearrange("p (c n) -> p c n", c=2), reads=[ot_t])
                    nout[0] += 1

            load(0)
            if n_ > 1:
                load(1)
            prep(0)
            for i in range(n_):
                if i + 1 < n_:
                    prep(i + 1)
                scan(i)
                if i + 2 < n_:
                    load(i + 2)
            S.barrier()
        A.release(m)

    def outproj_phase(self, l, src_x, src_xc, dst_x, dst_xc, need_ctx):
        S, A, dr = self.S, self.A, self.dr
        T = self.T
        m = A.mark()
        B, BT = self.banks, self.bank_t
        Wo = A.alloc(8 * D, BF16); Wo_t = Tl("Wo")
        Wo3 = Wo.rearrange("p (k n) -> p k n", k=8)
        wo = dr["mix_w_out"][l]
        S.dma(self.dsem("wold"), Wo3[:, 0:2, :], wo[0:256, :].rearrange("(j p) n -> p j n", p=128), writes=[Wo_t], q="pool")
        for g in range(4):
            S.dma(self.dsem("wold"), Wo3[0:64, 2 + g, :], wo[256 + g * 64: 256 + (g + 1) * 64, :], writes=[Wo_t], q="pool")
            S.dma(self.dsem("wold"), Wo3[64:128, 2 + g, :], wo[256 + (4 + g) * 64: 256 + (5 + g) * 64, :], writes=[Wo_t], q="pool")
        for g in range(2):
            S.dma(self.dsem("wold"), Wo3[0:64, 6 + g, :], wo[768 + g * 64: 768 + (g + 1) * 64, :], writes=[Wo_t], q="pool")
            S.dma(self.dsem("wold"), Wo3[64:128, 6 + g, :], wo[768 + (2 + g) * 64: 768 + (3 + g) * 64, :], writes=[Wo_t], q="pool")
        gB2 = A.alloc(2 * 1024, F32); gB2_t = Tl("gB2")
        for r in range(2):
            row = l * 6 + 2 + r
            S.dma(self.dsem("gld"), gB2[:, r * 1024:(r + 1) * 1024], dr["gates_d"][row:row + 1, :].partition_broadcast(128),
                  reads=[self.gateB_t[l]], writes=[gB2_t])
        NT = 2
        xts = [A.alloc(NT * 1024, F32) for _ in range(2)]; xts_t = [Tl("oxt0"), Tl("oxt1")]
        mT = [A.alloc(8 * 256, BF16) for _ in range(2)]; mT_t = [Tl("mT0"), Tl("mT1")]
        tmp = [A.alloc(512, F32) for _ in range(2)]; tmp_t = [Tl("otmp0"), Tl("otmp1")]
        tiles = []
        if need_ctx:
            tiles.append((src_xc, dst_xc, 0, 0, 2, 1))
        for i in range(T // 256):
            tiles.append((src_x, dst_x, i * 256, CTX + i * 256, 2, 0))

        def load(ti):
            src, dst, row0, tok0, nt, r = tiles[ti]
            b = ti % 2
            S.dma(self.dsem(f"oxld{b}"), xts[b][:, 0:nt * 1024].rearrange("p (t d) -> p t d", t=nt),
                  src[row0:row0 + nt * 128, :].rearrange("(t p) d -> p t d", p=128), writes=[xts_t[b]])
            S.dma(self.dsem(f"omld{b}"), mT[b].rearrange("p (c n) -> p c n", c=8), dr["mixT_d"][:, :, tok0:tok0 + nt * 128], writes=[mT_t[b]])
        load(0)
        on = 0
        for ti in range(len(tiles)):
            src, dst, row0, tok0, nt, r = tiles[ti]
            if ti + 1 < len(tiles):
                load(ti + 1)
            b = ti % 2
            N = nt * 128
            xt, xt_t = xts[b], xts_t[b]
            for s_ in range(nt):
                for hf in range(2):
                    bi = 6 + (on % 2)
                    tb = on % 2
                    on += 1
                    for k in range(8):
                        S.op("pe", lambda E, bi=bi, k=k, s_=s_, hf=hf, b=b: E.matmul(B[bi][:, :], mT[b][:, k * N + s_ * 128: k * N + (s_ + 1) * 128],
                                                                                    Wo3[:, k, hf * 512:(hf + 1) * 512], start=(k == 0), stop=(k == 7)),
                             reads=[mT_t[b], Wo_t], writes=[BT[bi]])
                    gB = gB2[:, r * 1024 + hf * 512: r * 1024 + (hf + 1) * 512]
                    S.op("dve", lambda E, bi=bi, tb=tb, gB=gB: E.tensor_tensor(out=tmp[tb], in0=B[bi][:, :], in1=gB, op=ALU.mult),
                         reads=[BT[bi], gB2_t], writes=[tmp_t[tb]])
                    xsl = xt[:, s_ * 1024 + hf * 512: s_ * 1024 + (hf + 1) * 512]
                    S.op("pool", lambda E, xsl=xsl, tb=tb: E.tensor_tensor(out=xsl, in0=tmp[tb], in1=xsl, op=ALU.add),
                         reads=[tmp_t[tb], xt_t], writes=[xt_t])
            S.dma(self.dsem(f"oxst{b}"), dst[row0:row0 + nt * 128, :].rearrange("(t p) d -> p t d", p=128),
                  xt[:, 0:nt * 1024].rearrange("p (t d) -> p t d", t=nt), reads=[xt_t])
        S.barrier()
        A.release(m)


_NC_CACHE = {}


def _get_nc(cfg_key, cfg):
    if cfg_key not in _NC_CACHE:
        _NC_CACHE[cfg_key] = build_nc(cfg)
    return _NC_CACHE[cfg_key]


def make_consts():
    c = np.zeros((128, NCONST), np.float32)
    c[:, C_ID:C_ID + 128] = np.eye(128, dtype=np.float32)
    s_ = np.arange(128)[:, None]
    t_ = np.arange(128)[None, :]
    same = (s_ // 64) == (t_ // 64)
    c[:, C_TRI + 0:C_TRI + 128] = (same & (s_ <= t_))
    c[:, C_TRI + 128:C_TRI + 256] = (same & (s_ > t_))
    c[:, C_TRI + 256:C_TRI + 384] = (same & (s_ >= t_))
    c[:, C_TRI + 384:C_TRI + 512] = (same & (s_ < t_))
    c[:, C_IND + 0] = (np.arange(128) < 64)
    c[:, C_IND + 1] = (np.arange(128) >= 64)
    c[:, C_WM:C_WM + 128] = (s_ >= t_)
    c[:, C_WM + 128:C_WM + 256] = (s_ <= t_)
    t64 = np.arange(64)[None, :]
    c[:, C_GM:C_GM + 64] = ((s_ % 64) <= t64)
    c[:, C_GM + 64:C_GM + 128] = ((s_ % 64) >= t64)
    c[:, C_BD:C_BD + 4] = ((np.arange(128)[:, None] // 32) == np.arange(4)[None, :])
    return c


def make_rope(T):
    t = np.arange(T)
    row = (t // 64).astype(np.float32)
    col = (t % 64).astype(np.float32)
    inv = np.power(np.float32(10000.0), -np.arange(16, dtype=np.float32) / np.float32(16)).astype(np.float32)
    ang = np.concatenate([row[:, None] * inv, col[:, None] * inv], axis=-1).astype(np.float32)
    return np.concatenate([np.cos(ang), np.sin(ang)], axis=-1).astype(np.float32)


WEIGHT_KEYS = ("mod_w", "mod_b", "norm_ffn1", "ffn1_w_in", "ffn1_w_out", "norm_ffn2", "ffn2_w_in", "ffn2_w_out",
               "norm_mix", "mix_w_in", "mix_w_out", "gla_wg_f", "gla_bg_f", "gla_wg_b", "gla_bg_b", "gla_out_norm",
               "glb_q_norm", "glb_k_norm", "win_q_norm", "win_k_norm", "win_sink")


def make_in_maps(inputs, T, depth):
    f = lambda a: np.ascontiguousarray(np.asarray(a, dtype=np.float32))
    x = f(inputs["x"]); c = f(inputs["c"]); ctx = f(inputs["ctx"]); c_ctx = f(inputs["c_ctx"])
    shared = {}
    for k in WEIGHT_KEYS:
        shared[k] = f(inputs[k])[:depth]
    shared["consts"] = make_consts()
    shared["rope"] = make_rope(T)
    maps = []
    for b in range(x.shape[0]):
        m = dict(shared)
        m["x"] = np.ascontiguousarray(x[b, :T])
        m["ctx"] = ctx[b]
        m["cvec"] = np.stack([c[b], c_ctx], axis=0)
        maps.append(m)
    return maps


def kernel(**inputs):
    cfg = Cfg()
    nc = _get_nc("full", cfg)
    maps = make_in_maps(inputs, cfg.T, cfg.depth)
    res = run_bass_kernel_spmd(nc, maps, core_ids=list(range(8)))
    return np.stack([r["out"] for r in res.results], axis=0)
```

```python
import contextlib
import numpy as np
import concourse.bass as bass
import concourse.mybir as mybir
from concourse.bass_utils import run_bass_kernel_spmd

F32 = mybir.dt.float32
BF16 = mybir.dt.bfloat16
AF = mybir.ActivationFunctionType
ALU = mybir.AluOpType
AX = mybir.AxisListType

D = 1024
CTX = 256
HID = 2816
NMOD = 9
INW = 2080
EPS = 1e-6
EPOCH = 30000
C_ID = 0
C_TRI = 128
C_IND = 640
C_WM = 642
C_GM = 898
C_BD = 1026
NCONST = 1032


class Tl:
    __slots__ = ("w", "r", "name")

    def __init__(self, name=""):
        self.w = None
        self.r = {}
        self.name = name


class Sched:
    ENGS = ("pe", "act", "dve", "pool", "sp")

    def __init__(self, nc, sem_pool):
        self.nc = nc
        self.free_sems = list(sem_pool)
        self.ops = {e: [] for e in self.ENGS}
        self.cnt = {e: 0 for e in self.ENGS}
        self.seen = {e: {} for e in self.ENGS}
        self.semh = {}
        self.semv = {}
        self.epoch = {e: 0 for e in self.ENGS}
        self.ninstr = 0
        for e in ("pe", "act", "dve", "pool"):
            self._new_epoch(e, first=True)

    def _alloc(self, key):
        h = self.free_sems.pop()
        self.semh[key] = h
        self.semv[key] = 0
        return key

    def _new_epoch(self, e, first=False):
        if not first:
            self.epoch[e] += 1
        self._alloc((e, self.epoch[e]))
        self.cnt[e] = 0

    def dma_sem(self, name):
        return self._alloc(("dma", name))

    def _wait(self, eng, key, val):
        if self.seen[eng].get(key, 0) >= val:
            return
        self.seen[eng][key] = val
        h = self.semh[key]
        self.ops[eng].append(lambda E, h=h, v=val: E.wait_ge(h, v))
        self.ninstr += 1

    def _deps(self, eng, reads, writes, ownkey):
        deps = {}
        raw = {}
        for t in reads:
            if t.w is not None:
                k, v = t.w
                if raw.get(k, 0) < v:
                    raw[k] = v
        for t in writes:
            if t.w is not None:
                k, v = t.w
                if deps.get(k, 0) < v:
                    deps[k] = v
            for k, v in t.r.items():
                if deps.get(k, 0) < v:
                    deps[k] = v
        for k, v in raw.items():
            if k == ownkey and eng == "pe":
                continue
            self._wait(eng, k, v)
        for k, v in deps.items():
            if k == ownkey and eng in ("pe", "sp", "pool_dma"):
                continue
            if k == ownkey and k[0] == "dma":
                continue
            self._wait(eng, k, v)

    def op(self, eng, fn, reads=(), writes=()):
        if self.cnt[eng] >= EPOCH:
            self._new_epoch(eng)
        own = (eng, self.epoch[eng])
        self._deps(eng, reads, writes, own)
        self.cnt[eng] += 1
        self.semv[own] = self.cnt[eng]
        h = self.semh[own]
        self.ops[eng].append(lambda E, fn=fn, h=h: fn(E).then_inc(h, 1))
        self.ninstr += 1
        dep = (own, self.cnt[eng])
        for t in reads:
            if t.r.get(own, 0) < dep[1]:
                t.r[own] = dep[1]
        for t in writes:
            t.w = dep
            t.r = {}

    def dma(self, semkey, out, in_, reads=(), writes=(), q="sp", **kw):
        self._deps(q, reads, writes, semkey)
        self.semv[semkey] += 16
        v = self.semv[semkey]
        h = self.semh[semkey]
        self.ops[q].append(lambda E, out=out, in_=in_, h=h, kw=kw: E.dma_start(out=out, in_=in_, **kw).then_inc(h, 16))
        self.ninstr += 1
        dep = (semkey, v)
        for t in reads:
            if t.r.get(semkey, 0) < v:
                t.r[semkey] = v
        for t in writes:
            t.w = dep
            t.r = {}

    def barrier(self):
        for e in self.ENGS:
            for k, v in self.semv.items():
                if v > 0:
                    self._wait(e, k, v)

    def final_wait(self, eng="sp"):
        for k, v in self.semv.items():
            if v > 0:
                self._wait(eng, k, v)


class Arena:
    def __init__(self, base_ap, nwords):
        self.base = base_ap
        self.n = nwords
        self.off = 0

    def alloc(self, nelem, dtype=F32):
        words = nelem if dtype == F32 else (nelem + 1) // 2
        words = (words + 7) // 8 * 8
        assert self.off + words <= self.n, f"SBUF arena overflow {self.off}+{words}>{self.n}"
        a = self.base[:, self.off:self.off + words]
        self.off += words
        if dtype != F32:
            a = a.bitcast(dtype)[:, :nelem]
        else:
            a = a[:, :nelem]
        return a

    def mark(self):
        return self.off

    def release(self, m):
        self.off = m


class Cfg:
    def __init__(self, T=8192, depth=2, stop_after=None, debug=False, skip=()):
        self.skip = set(skip)
        self.T = T
        self.depth = depth
        self.stop_after = stop_after
        self.debug = debug


def build_nc(cfg):
    T = cfg.T
    nc = bass.Bass("TRN2", target_bir_lowering=False)
    dr = {}

    def din(name, shape, dt=F32):
        dr[name] = nc.dram_tensor(name, list(shape), dt, kind="ExternalInput").ap()
        return dr[name]

    L = cfg.depth
    dk = dict(kind="ExternalOutput") if cfg.debug else {}
    x_in = din("x", [T, D])
    ctx_in = din("ctx", [CTX, D])
    cvec = din("cvec", [2, D])
    mod_w = din("mod_w", [L, D, NMOD * D])
    mod_b = din("mod_b", [L, NMOD * D])
    norm_ffn1 = din("norm_ffn1", [L, D])
    ffn1_w_in = din("ffn1_w_in", [L, D, 2 * HID])
    ffn1_w_out = din("ffn1_w_out", [L, HID, D])
    norm_ffn2 = din("norm_ffn2", [L, D])
    ffn2_w_in = din("ffn2_w_in", [L, D, 2 * HID])
    ffn2_w_out = din("ffn2_w_out", [L, HID, D])
    for nm, shp in (("norm_mix", [L, D]), ("mix_w_in", [L, D, INW]), ("mix_w_out", [L, D, D]),
                    ("gla_wg_f", [L, 16, 128]), ("gla_bg_f", [L, 128]), ("gla_wg_b", [L, 16, 128]), ("gla_bg_b", [L, 128]),
                    ("gla_out_norm", [L, 64]), ("glb_q_norm", [L, 64]), ("glb_k_norm", [L, 64]),
                    ("win_q_norm", [L, 64]), ("win_k_norm", [L, 64]), ("win_sink", [L, 4])):
        din(nm, shp)
    din("consts", [128, NCONST])
    din("rope", [T, 64])
    TC = T + CTX
    dr["qkT_d"] = nc.dram_tensor("qkT_d", [128, 8, TC], BF16, **dk).ap()
    dr["v_d"] = nc.dram_tensor("v_d", [TC, 512], BF16, **dk).ap()
    dr["gla_d"] = nc.dram_tensor("gla_d", [TC, 1024], F32, **dk).ap()
    dr["mixT_d"] = nc.dram_tensor("mixT_d", [128, 8, TC], BF16, **dk).ap()
    dr["of_d"] = nc.dram_tensor("of_d", [TC, 256], F32).ap()
    out_d = nc.dram_tensor("out", [T, D], F32, kind="ExternalOutput").ap()
    dk = dict(kind="ExternalOutput") if cfg.debug else {}
    xs = [nc.dram_tensor(f"xs{i}", [T, D], F32, **dk).ap() for i in range(2)]
    xcs = [nc.dram_tensor(f"xcs{i}", [CTX, D], F32).ap() for i in range(2)]
    dr["gates_d"] = nc.dram_tensor("gates_d", [L * 6, D], F32).ap()

    NW = 52000
    with contextlib.ExitStack() as es:
        arena_t = es.enter_context(nc.sbuf_tensor("arena", [128, NW], F32))
        bigs = [es.enter_context(nc.psum_tensor(f"pbig{i}", [128, 1024], F32)) for i in range(4)]
        banks = [bigs[i // 2][:, (i % 2) * 512:(i % 2 + 1) * 512] for i in range(8)]
        sems = [es.enter_context(nc.semaphore(f"s{i}")) for i in range(96)]
        S = Sched(nc, sems)
        A = Arena(arena_t[:], NW)
        P = Prog(nc, cfg, S, A, banks, dr, xs, xcs, out_d)
        P.bigs = bigs
        P.build()
        S.final_wait("sp")
        block = es.enter_context(nc.Block())

        @block.tensor
        def _(E):
            for f in S.ops["pe"]:
                f(E)

        @block.scalar
        def _(E):
            for f in S.ops["act"]:
                f(E)

        @block.vector
        def _(E):
            for f in S.ops["dve"]:
                f(E)

        @block.gpsimd
        def _(E):
            for f in S.ops["pool"]:
                f(E)

        @block.sync
        def _(E):
            for f in S.ops["sp"]:
                f(E)
    nc._n_instr = S.ninstr
    return nc


class Prog:
    def __init__(self, nc, cfg, S, A, banks, dr, xs, xcs, out_d):
        self.nc, self.cfg, self.S, self.A = nc, cfg, S, A
        self.banks = banks
        self.bank_t = [Tl(f"bank{i}") for i in range(8)]
        self.dr = dr
        self.xs, self.xcs, self.out_d = xs, xcs, out_d
        self.T = cfg.T
        self.dma_sems = {}

    def dsem(self, name):
        if name not in self.dma_sems:
            self.dma_sems[name] = self.S.dma_sem(name)
        return self.dma_sems[name]

    def build(self):
        S, A, nc = self.S, self.A, self.nc
        cfg = self.cfg
        self.consts = A.alloc(NCONST, F32)
        self.consts_t = Tl("consts")
        S.dma(self.dsem("const_id"), self.consts, self.dr["consts"], writes=[self.consts_t])
        self.ident = A.alloc(128, BF16)
        self.ident_t = Tl("ident")
        S.op("dve", lambda E: E.tensor_copy(out=self.ident, in_=self.consts[:, C_ID:C_ID + 128]), reads=[self.consts_t], writes=[self.ident_t])
        self.ones_f = A.alloc(128, F32)
        self.ones_t = Tl("ones")
        S.op("dve", lambda E: E.memset(self.ones_f, 1.0), writes=[self.ones_t])
        self.sc = A.alloc(16, F32)
        self.sc_t = Tl("sc")
        craw = A.alloc(16, F32)
        craw_t = Tl()
        S.dma(self.dsem("const_c"), craw.rearrange("p (r j) -> p r j", r=2),
              self.dr["cvec"].rearrange("r (p j) -> p r j", j=8), writes=[craw_t])
        S.op("act", lambda E: E.activation(out=self.sc, in_=craw, func=AF.Silu), reads=[craw_t], writes=[self.sc_t])
        self.modcols = []
        self.modcols_t = []
        self.gateB_t = []
        for l in range(cfg.depth):
            self.modcols.append(A.alloc(6 * 8 * 2, F32))
            self.modcols_t.append(Tl(f"modcols{l}"))
            self.gateB_t.append(Tl(f"gateB{l}"))
        self.persist_mark = A.mark()
        xs, xcs = self.xs, self.xcs
        cur_x, cur_xc = self.dr["x"], self.dr["ctx"]
        for l in range(cfg.depth):
            last = (l == cfg.depth - 1)
            need_ctx = not last
            self.mod_phase(l)
            S.barrier()
            A.release(self.persist_mark)
            if cfg.stop_after == ("ffn1", l):
                self.ffn_phase(l, 0, cur_x, cur_xc, self.out_d, xcs[0])
                return
            sk = cfg.skip
            self.ffn_phase(l, 0, cur_x, cur_xc, xs[0], xcs[0])
            if "inproj" not in sk:
                self.inproj_phase(l, xs[0], xcs[0])
            if "gattn" not in sk:
                self.gattn_phase(l, need_ctx)
            if "wattn" not in sk:
                self.wattn_phase(l, need_ctx)
            if "gla" not in sk:
                self.gla_phase(l, need_ctx)
            self.outproj_phase(l, xs[0], xcs[0], xs[1], xcs[1], need_ctx)
            dst = self.out_d if last else xs[0]
            self.ffn_phase(l, 1, xs[1], xcs[1], dst, xcs[0], do_ctx=need_ctx)
            cur_x, cur_xc = xs[0], xcs[0]

    def mod_phase(self, l):
        S, A, nc = self.S, self.A, self.nc
        m = A.mark()
        W = [A.alloc(8 * 1024, F32) for _ in range(2)]
        W_t = [Tl(f"modW{i}") for i in range(2)]
        brow = A.alloc(NMOD * D, F32)
        brow_t = Tl("brow")
        S.dma(self.dsem("modb"), brow[0:1, :], self.dr["mod_b"][l:l + 1, :], writes=[brow_t])
        scB = A.alloc(2 * 8 * 128, F32)
        scB_t = Tl("scB")
        grow = [A.alloc(512, F32) for _ in range(2)]
        grow_t = [Tl("grow0"), Tl("grow1")]
        ones = self.ones_f
        for r in range(2):
            for j in range(8):
                o = scB[:, (r * 8 + j) * 128:(r * 8 + j + 1) * 128]
                s1 = self.sc[:, r * 8 + j:r * 8 + j + 1]
                S.op("dve", lambda E, o=o, s1=s1: E.tensor_scalar(out=o, in0=ones, scalar1=s1, scalar2=None, op0=ALU.mult),
                     reads=[self.sc_t, self.ones_t], writes=[scB_t])
        colvec = {0: 0, 1: 1, 3: 2, 4: 3, 6: 4, 7: 5}
        gatevec = {2: 0, 5: 1, 8: 2}
        modw = self.dr["mod_w"]
        for v in range(NMOD):
            wb = W[v % 2]
            wt = W_t[v % 2]
            S.dma(self.dsem(f"modw{v % 2}"), wb.rearrange("p (j n) -> p j n", j=8),
                  modw[l, :, v * D:(v + 1) * D].rearrange("(p j) n -> p j n", j=8), writes=[wt])
            if v in colvec:
                ci = colvec[v]
                bk = self.banks[0]
                bt = self.bank_t[0]
                for c in range(8):
                    for j in range(8):
                        lhsT = wb[:, j * 1024 + c * 128: j * 1024 + (c + 1) * 128]
                        rhs = self.sc.rearrange("p (r j) -> p j r", r=2)[:, j, :]
                        S.op("pe", lambda E, o=bk[:, c * 2:c * 2 + 2], lhsT=lhsT, rhs=rhs, j=j:
                             E.matmul(o, lhsT, rhs, start=(j == 0), stop=False),
                             reads=[wt, self.sc_t], writes=[bt])
                    lb = brow[0:1, v * D + c * 128: v * D + (c + 1) * 128]
                    S.op("pe", lambda E, o=bk[:, c * 2:c * 2 + 2], lb=lb:
                         E.matmul(o, lb, ones[0:1, 0:2], start=False, stop=True),
                         reads=[brow_t, self.ones_t], writes=[bt])
                dst = self.modcols[l][:, ci * 16:(ci + 1) * 16]
                S.op("dve", lambda E, dst=dst, bk=bk: E.tensor_copy(out=dst, in_=bk[:, 0:16]),
                     reads=[bt], writes=[self.modcols_t[l]])
            else:
                gi = gatevec[v]
                for r in range(2):
                    for hf in range(2):
                        bi = 1 + (r * 2 + hf)
                        bk, bt = self.banks[bi], self.bank_t[bi]
                        for j in range(8):
                            lhsT = scB[:, (r * 8 + j) * 128:(r * 8 + j + 1) * 128]
                            rhs = wb[:, j * 1024 + hf * 512: j * 1024 + (hf + 1) * 512]
                            S.op("pe", lambda E, bk=bk, lhsT=lhsT, rhs=rhs, j=j:
                                 E.matmul(bk[:, :], lhsT, rhs, start=(j == 0), stop=False),
                                 reads=[wt, scB_t], writes=[bt])
                        rb = brow[0:1, v * D + hf * 512: v * D + (hf + 1) * 512]
                        S.op("pe", lambda E, bk=bk, rb=rb: E.matmul(bk[:, :], ones[0:1, :], rb, start=False, stop=True),
                             reads=[brow_t, self.ones_t], writes=[bt])
                        gb = grow[(r * 2 + hf) % 2]
                        gbt = grow_t[(r * 2 + hf) % 2]
                        sc_ = 0.5 if v in (2, 8) else 1.0
                        S.op("act", lambda E, gb=gb, bk=bk, sc_=sc_: E.activation(out=gb, in_=bk[:, :], func=AF.Copy, scale=sc_),
                             reads=[bt], writes=[gbt])
                        row = l * 6 + gi * 2 + r
                        S.dma(self.dsem(f"gst{(r * 2 + hf) % 2}"), self.dr["gates_d"][row:row + 1, hf * 512:(hf + 1) * 512], gb[0:1, :],
                              reads=[gbt], writes=[self.gateB_t[l]])
        A.release(m)

    def make_GS(self, l, norm_name, vec_shift_ci, vec_scale_ci):
        S, A = self.S, self.A
        gs = A.alloc(32, F32)
        gs_t = Tl("gs")
        gcol = A.alloc(8, F32)
        gcol_t = Tl("gcol")
        with self.nc.allow_non_contiguous_dma(reason="tiny norm gain column load"):
            pass
        S.dma(self.dsem("gcol"), gcol, self.dr[norm_name][l, :].rearrange("(j p) -> p j", p=128), writes=[gcol_t],
              allow_slow_non_contiguous=True)
        mc = self.modcols[l].rearrange("p (v c r) -> p v r c", v=6, c=8, r=2)
        for r in range(2):
            go = gs[:, (r * 2) * 8:(r * 2 + 1) * 8]
            so = gs[:, (r * 2 + 1) * 8:(r * 2 + 2) * 8]
            S.op("dve", lambda E, go=go, r=r: E.scalar_tensor_tensor(out=go, in0=mc[:, vec_scale_ci, r, :], scalar=1.0, in1=gcol,
                                                                    op0=ALU.add, op1=ALU.mult),
                 reads=[self.modcols_t[l], gcol_t], writes=[gs_t])
            S.op("dve", lambda E, so=so, r=r: E.tensor_copy(out=so, in_=mc[:, vec_shift_ci, r, :]),
                 reads=[self.modcols_t[l]], writes=[gs_t])
        return gs, gs_t

    def norm_to_hT(self, xt, xt_t, nt, gs, gs_t, r, xn, xn_t, hT, hT_t, ss, ss_t, junk, junk_t, tbanks, part="both"):
        S = self.S
        N = nt * 128
        if part in ("both", "norm"):
            self._norm_part(xt, xt_t, nt, xn, xn_t, ss, ss_t, junk, junk_t)
        if part in ("both", "tr"):
            self._tr_part(nt, gs, gs_t, r, xn, xn_t, hT, hT_t, tbanks)

    def _norm_part(self, xt, xt_t, nt, xn, xn_t, ss, ss_t, junk, junk_t):
        S = self.S
        N = nt * 128
        for s in range(nt):
            xin = xt[:, s * 1024:(s + 1) * 1024]
            S.op("act", lambda E, xin=xin, s=s: E.activation(out=junk, in_=xin, func=AF.Square, accum_out=ss[:, s:s + 1]),
                 reads=[xt_t], writes=[junk_t, ss_t])
        S.op("dve", lambda E: E.tensor_scalar(out=ss[:, 4:4 + nt], in0=ss[:, 0:nt], scalar1=1.0 / D, scalar2=EPS,
                                              op0=ALU.mult, op1=ALU.add), reads=[ss_t], writes=[ss_t])
        S.op("act", lambda E: E.activation(out=ss[:, 8:8 + nt], in_=ss[:, 4:4 + nt], func=AF.Sqrt), reads=[ss_t], writes=[ss_t])
        S.op("dve", lambda E: E.reciprocal(out=ss[:, 12:12 + nt], in_=ss[:, 8:8 + nt]), reads=[ss_t], writes=[ss_t])
        for s in range(nt):
            xin = xt[:, s * 1024:(s + 1) * 1024]
            xo = xn[:, s * 1024:(s + 1) * 1024]
            eng = "dve" if s % 2 == 0 else "pool"
            S.op(eng, lambda E, xin=xin, xo=xo, s=s: E.tensor_scalar(out=xo, in0=xin, scalar1=ss[:, 12 + s:13 + s], scalar2=None,
                                                                   op0=ALU.mult), reads=[xt_t, ss_t], writes=[xn_t])

    def _tr_part(self, nt, gs, gs_t, r, xn, xn_t, hT, hT_t, tbanks):
        S = self.S
        N = nt * 128
        for j in range(8):
            bi = tbanks[j % len(tbanks)]
            bk = self.banks[bi][:, :].bitcast(BF16)
            bt = self.bank_t[bi]
            for s in range(nt):
                S.op("pe", lambda E, bk=bk, s=s, j=j: E.transpose(out=bk[:, s * 128:(s + 1) * 128],
                                                                 in_=xn[:, s * 1024 + j * 128: s * 1024 + (j + 1) * 128],
                                                                 identity=self.ident),
                     reads=[xn_t, self.ident_t], writes=[bt])
            gcol = gs[:, (r * 2) * 8 + j:(r * 2) * 8 + j + 1]
            scol = gs[:, (r * 2 + 1) * 8 + j:(r * 2 + 1) * 8 + j + 1]
            S.op("act", lambda E, bk=bk, j=j, gcol=gcol, scol=scol: E.activation(out=hT[:, j * N:(j + 1) * N], in_=bk[:, 0:N],
                                                                               func=AF.Identity, bias=scol, scale=gcol),
                 reads=[bt, gs_t], writes=[hT_t])

    def ffn_phase(self, l, which, src_x, src_xc, dst_x, dst_xc, do_ctx=True):
        S, A, nc = self.S, self.A, self.nc
        T = self.T
        m = A.mark()
        wname = "ffn1" if which == 0 else "ffn2"
        Win = A.alloc(8 * 2 * HID, BF16)
        Win_t = Tl("Win")
        Wout = A.alloc(22 * D, BF16)
        Wout_t = Tl("Wout")
        w_in_d = self.dr[wname + "_w_in"][l].rearrange("(j p) n -> p j n", p=128)
        w_out_d = self.dr[wname + "_w_out"][l].rearrange("(j p) n -> p j n", p=128)
        for j in range(8):
            S.dma(self.dsem("wld_in"), Win[:, j * 2 * HID:(j + 1) * 2 * HID], w_in_d[:, j, :], writes=[Win_t], q="pool")
        for j in range(22):
            S.dma(self.dsem("wld_out"), Wout[:, j * D:(j + 1) * D], w_out_d[:, j, :], writes=[Wout_t], q="pool")
        if which == 0:
            gs, gs_t = self.make_GS(l, "norm_ffn1", 0, 1)
            gi = 0
        else:
            gs, gs_t = self.make_GS(l, "norm_ffn2", 4, 5)
            gi = 2
        NT = 2
        gB2 = A.alloc(2 * 1024, F32)
        gB2_t = Tl("gB2")
        for r in range(2):
            row = l * 6 + gi * 2 + r
            S.dma(self.dsem("gld"), gB2[:, r * 1024:(r + 1) * 1024], self.dr["gates_d"][row:row + 1, :].partition_broadcast(128),
                  reads=[self.gateB_t[l]], writes=[gB2_t])
        xts = [A.alloc(NT * 1024, F32) for _ in range(2)]
        xts_t = [Tl(f"xt{i}") for i in range(2)]
        xn = A.alloc(NT * 1024, BF16); xn_t = Tl("xn")
        hT = A.alloc(8 * NT * 128, BF16); hT_t = Tl("hT")
        g = A.alloc(22 * NT * 128, BF16); g_t = Tl("g")
        ss = A.alloc(16, F32); ss_t = Tl("ss")
        junk = A.alloc(1024, BF16); junk_t = Tl("junk")
        sa = [A.alloc(NT * 128, F32) for _ in range(2)]
        sa_t = [Tl(f"sa{i}") for i in range(2)]
        tmp = [A.alloc(512, F32) for _ in range(2)]
        tmp_t = [Tl(f"tmp{i}") for i in range(2)]
        tiles = []
        if do_ctx:
            tiles.append((src_xc, dst_xc, 0, 2, 1))
        for i in range(T // 256):
            tiles.append((src_x, dst_x, i * 256, 2, 0))
        def load(ti):
            src, dst, row0, nt, r = tiles[ti]
            b = ti % 2
            S.dma(self.dsem(f"xld{b}"), xts[b][:, 0:nt * 1024].rearrange("p (t d) -> p t d", t=nt),
                  src[row0:row0 + nt * 128, :].rearrange("(t p) d -> p t d", p=128), writes=[xts_t[b]])
        load(0)
        abn = 0
        on = 0
        if len(tiles) > 1:
            load(1)
        _s0, _d0, _r0, nt0, r0 = tiles[0]
        self.norm_to_hT(xts[0], xts_t[0], nt0, gs, gs_t, r0, xn, xn_t, hT, hT_t, ss, ss_t, junk, junk_t, tbanks=[0, 1])
        for ti in range(len(tiles)):
            src, dst, row0, nt, r = tiles[ti]
            b = ti % 2
            xt, xt_t = xts[b], xts_t[b]
            N = nt * 128
            for hc in range(22):
                pa = 2 + (abn % 2) * 2
                abn += 1
                bka, bta = self.banks[pa], self.bank_t[pa]
                bkb, btb = self.banks[pa + 1], self.bank_t[pa + 1]
                for (bk, bt, c0) in ((bka, bta, hc * 128), (bkb, btb, HID + hc * 128)):
                    for k in range(8):
                        lhsT = Win[:, k * 2 * HID + c0: k * 2 * HID + c0 + 128]
                        rhs = hT[:, k * N:(k + 1) * N]
                        S.op("pe", lambda E, bk=bk, lhsT=lhsT, rhs=rhs, k=k: E.matmul(bk[:, 0:N], lhsT, rhs, start=(k == 0), stop=(k == 7)),
                             reads=[Win_t, hT_t], writes=[bt])
                sb = hc % 2
                S.op("act", lambda E, bka=bka, sb=sb: E.activation(out=sa[sb][:, 0:N], in_=bka[:, 0:N], func=AF.Silu),
                     reads=[bta], writes=[sa_t[sb]])
                S.op("dve", lambda E, bkb=bkb, sb=sb, hc=hc: E.tensor_tensor(out=g[:, hc * N:(hc + 1) * N], in0=bkb[:, 0:N], in1=sa[sb][:, 0:N], op=ALU.mult),
                     reads=[btb, sa_t[sb]], writes=[g_t])
            if ti + 1 < len(tiles):
                _s1, _d1, _r1, nt1, r1 = tiles[ti + 1]
                self.norm_to_hT(xts[(ti + 1) % 2], xts_t[(ti + 1) % 2], nt1, gs, gs_t, r1, xn, xn_t, hT, hT_t, ss, ss_t, junk, junk_t,
                                tbanks=[0, 1], part="norm")
            for s in range(nt):
                for hf in range(2):
                    bi = 6 + (on % 2)
                    tb = on % 2
                    on += 1
                    bk, bt = self.banks[bi], self.bank_t[bi]
                    for hc in range(22):
                        lhsT = g[:, hc * N + s * 128: hc * N + (s + 1) * 128]
                        rhs = Wout[:, hc * D + hf * 512: hc * D + (hf + 1) * 512]
                        S.op("pe", lambda E, bk=bk, lhsT=lhsT, rhs=rhs, hc=hc: E.matmul(bk[:, :], lhsT, rhs, start=(hc == 0), stop=(hc == 21)),
                             reads=[g_t, Wout_t], writes=[bt])
                    gB = gB2[:, r * 1024 + hf * 512: r * 1024 + (hf + 1) * 512]
                    S.op("dve", lambda E, bk=bk, tb=tb, gB=gB: E.tensor_tensor(out=tmp[tb], in0=bk[:, :], in1=gB, op=ALU.mult),
                         reads=[bt, gB2_t], writes=[tmp_t[tb]])
                    xsl = xt[:, s * 1024 + hf * 512: s * 1024 + (hf + 1) * 512]
                    S.op("pool", lambda E, xsl=xsl, tb=tb: E.tensor_tensor(out=xsl, in0=tmp[tb], in1=xsl, op=ALU.add),
                         reads=[tmp_t[tb], xt_t], writes=[xt_t])
            S.dma(self.dsem(f"xst{b}"), dst[row0:row0 + nt * 128, :].rearrange("(t p) d -> p t d", p=128),
                  xt[:, 0:nt * 1024].rearrange("p (t d) -> p t d", t=nt), reads=[xt_t])
            if ti + 2 < len(tiles):
                load(ti + 2)
            if ti + 1 < len(tiles):
                _s1, _d1, _r1, nt1, r1 = tiles[ti + 1]
                self.norm_to_hT(xts[(ti + 1) % 2], xts_t[(ti + 1) % 2], nt1, gs, gs_t, r1, xn, xn_t, hT, hT_t, ss, ss_t, junk, junk_t,
                                tbanks=[0, 1], part="tr")
        S.barrier()
        A.release(m)

    def inproj_phase(self, l, src_x, src_xc):
        S, A, nc = self.S, self.A, self.nc
        T = self.T
        m = A.mark()
        dr = self.dr
        cst = self.consts
        Wm = A.alloc(8 * INW, BF16); Wm_t = Tl("Wm")
        Wm3 = Wm.rearrange("p (j n) -> p j n", j=8)
        wd = dr["mix_w_in"][l].rearrange("(j p) n -> p j n", p=128)

        def wl(dst0, src0, n):
            S.dma(self.dsem("wmix"), Wm3[:, :, dst0:dst0 + n], wd[:, :, src0:src0 + n], writes=[Wm_t], q="pool")
        wl(0, 0, 800)
        for g in range(4):
            wl(800 + (2 * g) * 64, 800 + g * 64, 64)
            wl(800 + (2 * g + 1) * 64, 800 + (4 + g) * 64, 64)
        wl(800 + 512, 1312, 128)
        for g in range(2):
            wl(800 + 640 + (2 * g) * 64, 1568 + g * 64, 64)
            wl(800 + 640 + (2 * g + 1) * 64, 1568 + (2 + g) * 64, 64)
        wl(800 + 896, 1824, 128)
        wl(1824, 1440, 128)
        wl(1952, 1952, 128)
        gs, gs_t = self.make_GS(l, "norm_mix", 2, 3)
        gains = A.alloc(256, F32); gains_t = Tl("gains")
        for i, nm in enumerate(("glb_q_norm", "glb_k_norm", "win_q_norm", "win_k_norm")):
            S.dma(self.dsem("gains"), gains[:, i * 64:(i + 1) * 64], dr[nm][l:l + 1, :].partition_broadcast(128), writes=[gains_t])
        for i in (0, 2):
            gv_ = gains[:, i * 64:(i + 1) * 64]
            S.op("dve", lambda E, gv_=gv_: E.tensor_scalar(out=gv_, in0=gv_, scalar1=0.125, scalar2=None, op0=ALU.mult),
                 reads=[gains_t], writes=[gains_t])
        Wg = A.alloc(256, F32); Wg_t = Tl("Wg")
        S.op("dve", lambda E: E.memset(Wg[0:32, :], 0.0), writes=[Wg_t])
        S.dma(self.dsem("wg"), Wg[0:16, 0:128], dr["gla_wg_f"][l], writes=[Wg_t])
        S.dma(self.dsem("wg"), Wg[16:32, 128:256], dr["gla_wg_b"][l], writes=[Wg_t])
        bgr = A.alloc(256, F32); bgr_t = Tl("bgr")
        S.dma(self.dsem("bg"), bgr[0:1, 0:128], dr["gla_bg_f"][l:l + 1, :], writes=[bgr_t])
        S.dma(self.dsem("bg"), bgr[0:1, 128:256], dr["gla_bg_b"][l:l + 1, :], writes=[bgr_t])
        NT = 2
        xts = [A.alloc(NT * 1024, F32) for _ in range(2)]
        xts_t = [Tl(f"ixt{i}") for i in range(2)]
        xn = A.alloc(NT * 1024, BF16); xn_t = Tl("xn")
        hT = A.alloc(8 * NT * 128, BF16); hT_t = Tl("hT")
        ss = A.alloc(16, F32); ss_t = Tl("ss")
        junk = A.alloc(1024, BF16); junk_t = Tl("junk")
        adT = A.alloc(256, F32); adT_t = Tl("adT")
        gst = [A.alloc(1024, F32) for _ in range(2)]; gst_t = [Tl("gst0"), Tl("gst1")]
        vst = [A.alloc(512, BF16) for _ in range(2)]; vst_t = [Tl("vst0"), Tl("vst1")]
        for i in range(2):
            S.op("pool", lambda E, i=i: E.memset(vst[i], 1.0), writes=[vst_t[i]])
        sqs = [A.alloc(1024, F32) for _ in range(2)]; sqs_t = [Tl("sq0"), Tl("sq1")]
        sts = [A.alloc(64, F32) for _ in range(2)]; sts_t = [Tl("st0"), Tl("st1")]
        qks = [A.alloc(1024, F32) for _ in range(2)]; qks_t = [Tl("qk0"), Tl("qk1")]
        qraws = [A.alloc(1024, F32) for _ in range(2)]; qraws_t = [Tl("qraw0"), Tl("qraw1")]
        rts = [[A.alloc(512, F32) for _ in range(4)] for _ in range(2)]
        rts_t = [[Tl(f"rt{p}{i}") for i in range(4)] for p in range(2)]
        qkbs = [A.alloc(1024, BF16) for _ in range(2)]; qkbs_t = [Tl("qkb0"), Tl("qkb1")]
        qkT = [A.alloc(8 * 256, BF16) for _ in range(2)]; qkT_t = [Tl("qkT0"), Tl("qkT1")]
        cs = [A.alloc(64, F32) for _ in range(2)]; cs_t = [Tl("cs0"), Tl("cs1")]
        egs = [A.alloc(256, F32) for _ in range(2)]; egs_t = [Tl("eg0"), Tl("eg1")]
        B, BT = self.banks, self.bank_t
        tiles = [(src_xc, 0, 0, 2, 1, None)]
        for i in range(T // 256):
            tiles.append((src_x, i * 256, CTX + i * 256, 2, 0, i * 256))

        def load(ti):
            src, row0, tok0, nt, r, lat0 = tiles[ti]
            b = ti % 2
            S.dma(self.dsem(f"ixld{b}"), xts[b][:, 0:nt * 1024].rearrange("p (t d) -> p t d", t=nt),
                  src[row0:row0 + nt * 128, :].rearrange("(t p) d -> p t d", p=128), writes=[xts_t[b]])
        subs = []
        for ti in range(len(tiles)):
            for s_ in range(tiles[ti][3]):
                subs.append((ti, s_))
        N = 256

        def stageA(n):
            ti, s_ = subs[n]
            src, row0, tok0, nt, r, lat0 = tiles[ti]
            sb = n % 2
            b = ti % 2
            if s_ == 0:
                if ti + 1 < len(tiles):
                    load(ti + 1)
                self.norm_to_hT(xts[b], xts_t[b], nt, gs, gs_t, r, xn, xn_t, hT, hT_t, ss, ss_t, junk, junk_t, tbanks=[0, 1])
                for k in range(8):
                    S.op("pe", lambda E, k=k: E.matmul(B[6][0:32, 256:256 + N], Wm3[:, k, 768:800], hT[:, k * N:(k + 1) * N], start=(k == 0), stop=(k == 7)),
                         reads=[Wm_t, hT_t], writes=[BT[6]])
                S.op("act", lambda E: E.activation(out=adT[0:32, 0:N], in_=B[6][0:32, 256:256 + N], func=AF.Copy), reads=[BT[6]], writes=[adT_t])
            tk = tok0 + s_ * 128
            if lat0 is not None:
                S.dma(self.dsem(f"cs{sb}"), cs[sb], dr["rope"][lat0 + s_ * 128: lat0 + (s_ + 1) * 128, :], writes=[cs_t[sb]])
            for (bi, c0, n_) in ((4, 800, 512), (5, 1312, 512), (2, 0, 512), (3, 512, 256), (6, 1824, 256)):
                for k in range(8):
                    S.op("pe", lambda E, bi=bi, c0=c0, n_=n_, k=k, s_=s_: E.matmul(B[bi][:, 0:n_], hT[:, k * N + s_ * 128: k * N + (s_ + 1) * 128],
                                                                                   Wm3[:, k, c0:c0 + n_], start=(k == 0), stop=(k == 7)),
                         reads=[Wm_t, hT_t], writes=[BT[bi]])
            S.op("pe", lambda E, s_=s_: E.matmul(B[3][:, 256:512], adT[0:32, s_ * 128:(s_ + 1) * 128], Wg[0:32, :], start=True, stop=False),
                 reads=[adT_t, Wg_t], writes=[BT[3]])
            S.op("pe", lambda E: E.matmul(B[3][:, 256:512], self.ones_f[0:1, :], bgr[0:1, :], start=False, stop=True),
                 reads=[self.ones_t, bgr_t], writes=[BT[3]])
            qr = qraws[sb]; qr_t = qraws_t[sb]
            S.op("act", lambda E, qr=qr: E.activation(out=qr[:, 0:512], in_=B[4][:, :], func=AF.Copy), reads=[BT[4]], writes=[qr_t])
            S.op("dve", lambda E, qr=qr: E.tensor_copy(out=qr[:, 512:1024], in_=B[5][:, :]), reads=[BT[5]], writes=[qr_t])
            g_ = gst[sb]; g_t = gst_t[sb]
            eg = egs[sb]; eg_t = egs_t[sb]
            S.op("act", lambda E, g_=g_: E.activation(out=g_[:, 0:512], in_=B[2][:, :], func=AF.Copy), reads=[BT[2]], writes=[g_t])
            S.op("dve", lambda E, g_=g_: E.tensor_copy(out=g_[:, 512:768], in_=B[3][:, 0:256]), reads=[BT[3]], writes=[g_t])
            S.op("act", lambda E, eg=eg: E.activation(out=eg, in_=B[3][:, 256:512], func=AF.Exp, scale=-1.0), reads=[BT[3]], writes=[eg_t])
            v_ = vst[sb]; v_t = vst_t[sb]
            S.op("dve", lambda E, v_=v_: E.tensor_copy(out=v_.rearrange("p (a c) -> p a c", a=4)[:, :, 0:64],
                                                      in_=B[6][:, 0:256].rearrange("p (a c) -> p a c", a=4)),
                 reads=[BT[6]], writes=[v_t])
            S.dma(self.dsem(f"vstst{sb}"), dr["v_d"][tk:tk + 128, :], v_, reads=[v_t])
            S.op("act", lambda E, eg=eg: E.activation(out=eg, in_=eg, func=AF.Ln, bias=1.0), reads=[eg_t], writes=[eg_t])
            S.op("pool", lambda E, g_=g_, eg=eg: E.tensor_scalar(out=g_[:, 768:1024], in0=eg, scalar1=-1.0 / 16.0, scalar2=None, op0=ALU.mult),
                 reads=[eg_t], writes=[g_t])
            S.dma(self.dsem(f"gstst{sb}"), dr["gla_d"][tk:tk + 128, :], g_, reads=[g_t])

        def stageB(n):
            ti, s_ = subs[n]
            src, row0, tok0, nt, r, lat0 = tiles[ti]
            sb = n % 2
            qr = qraws[sb]; qr_t = qraws_t[sb]
            sq = sqs[sb]; sq_t = sqs_t[sb]
            st = sts[sb]; st_t = sts_t[sb]
            qk = qks[sb]; qk_t = qks_t[sb]
            rt = rts[sb]; rt_t = rts_t[sb]
            qkb = qkbs[sb]; qkb_t = qkbs_t[sb]
            S.op("act", lambda E: E.activation(out=sq, in_=qr, func=AF.Square), reads=[qr_t], writes=[sq_t])
            S.op("dve", lambda E: E.tensor_reduce(out=st[:, 0:16], in_=sq.rearrange("p (h d) -> p h d", d=64), axis=AX.X, op=ALU.add),
                 reads=[sq_t], writes=[st_t])
            S.op("dve", lambda E: E.tensor_scalar(out=st[:, 16:32], in0=st[:, 0:16], scalar1=1.0 / 64, scalar2=EPS, op0=ALU.mult, op1=ALU.add),
                 reads=[st_t], writes=[st_t])
            S.op("act", lambda E: E.activation(out=st[:, 32:48], in_=st[:, 16:32], func=AF.Sqrt), reads=[st_t], writes=[st_t])
            S.op("dve", lambda E: E.reciprocal(out=st[:, 48:64], in_=st[:, 32:48]), reads=[st_t], writes=[st_t])
            qk3 = qk.rearrange("p (h d) -> p h d", d=64)
            S.op("dve", lambda E: E.tensor_tensor(out=qk3, in0=qr.rearrange("p (h d) -> p h d", d=64),
                                                 in1=st[:, 48:64].unsqueeze(2).to_broadcast([128, 16, 64]), op=ALU.mult),
                 reads=[qr_t, st_t], writes=[qk_t])
            for (h0, h1, gi_) in ((0, 8, 0), (8, 10, 1), (10, 14, 2), (14, 16, 3)):
                S.op("pool", lambda E, h0=h0, h1=h1, gi_=gi_: E.tensor_tensor(
                    out=qk3[:, h0:h1, :], in0=qk3[:, h0:h1, :],
                    in1=gains[:, gi_ * 64:(gi_ + 1) * 64].unsqueeze(1).to_broadcast([128, h1 - h0, 64]), op=ALU.mult),
                    reads=[qk_t, gains_t], writes=[qk_t])
            qkb3 = qkb.rearrange("p (h d) -> p h d", d=64)
            if lat0 is not None:
                cosb = cs[sb][:, 0:32].unsqueeze(1).to_broadcast([128, 16, 32])
                sinb = cs[sb][:, 32:64].unsqueeze(1).to_broadcast([128, 16, 32])
                x1 = qk3[:, :, 0:32]; x2 = qk3[:, :, 32:64]
                r3 = [t_.rearrange("p (h d) -> p h d", d=32) for t_ in rt]
                S.op("dve", lambda E: E.tensor_tensor(out=r3[0], in0=x1, in1=cosb, op=ALU.mult), reads=[qk_t, cs_t[sb]], writes=[rt_t[0]])
                S.op("pool", lambda E: E.tensor_tensor(out=r3[1], in0=x2, in1=sinb, op=ALU.mult), reads=[qk_t, cs_t[sb]], writes=[rt_t[1]])
                S.op("pool", lambda E: E.tensor_tensor(out=r3[2], in0=x1, in1=sinb, op=ALU.mult), reads=[qk_t, cs_t[sb]], writes=[rt_t[2]])
                S.op("dve", lambda E: E.tensor_tensor(out=r3[3], in0=x2, in1=cosb, op=ALU.mult), reads=[qk_t, cs_t[sb]], writes=[rt_t[3]])
                S.op("dve", lambda E: E.tensor_tensor(out=qkb3[:, :, 0:32], in0=r3[0], in1=r3[1], op=ALU.subtract),
                     reads=[rt_t[0], rt_t[1]], writes=[qkb_t])
                S.op("pool", lambda E: E.tensor_tensor(out=qkb3[:, :, 32:64], in0=r3[2], in1=r3[3], op=ALU.add),
                     reads=[rt_t[2], rt_t[3]], writes=[qkb_t])
            else:
                S.op("act", lambda E: E.activation(out=qkb, in_=qk, func=AF.Copy), reads=[qk_t], writes=[qkb_t])

        def stageC(n):
            ti, s_ = subs[n]
            src, row0, tok0, nt, r, lat0 = tiles[ti]
            sb = n % 2
            qkb = qkbs[sb]; qkb_t = qkbs_t[sb]
            qT = qkT[ti % 2]; qT_t = qkT_t[ti % 2]
            for half in range(2):
                bi = half
                bkT = B[bi][:, :].bitcast(BF16)
                for c in range(4):
                    cc = half * 4 + c
                    S.op("pe", lambda E, bkT=bkT, c=c, cc=cc: E.transpose(out=bkT[:, c * 128:(c + 1) * 128], in_=qkb[:, cc * 128:(cc + 1) * 128],
                                                                      identity=self.ident),
                         reads=[qkb_t, self.ident_t], writes=[BT[bi]])
                dstv = qT.rearrange("p (c n) -> p c n", c=8)[:, half * 4:(half + 1) * 4, s_ * 128:(s_ + 1) * 128]
                srcv = bkT[:, 0:512].rearrange("p (c n) -> p c n", c=4)
                if half == 0:
                    S.op("act", lambda E, dstv=dstv, srcv=srcv: E.activation(out=dstv, in_=srcv, func=AF.Copy), reads=[BT[bi]], writes=[qT_t])
                else:
                    S.op("dve", lambda E, dstv=dstv, srcv=srcv: E.tensor_copy(out=dstv, in_=srcv), reads=[BT[bi]], writes=[qT_t])
            if s_ == nt - 1:
                S.dma(self.dsem(f"qkTst{ti % 2}"), dr["qkT_d"][:, :, tok0:tok0 + N], qT.rearrange("p (c n) -> p c n", c=8)[:, :, 0:N], reads=[qT_t])

        load(0)
        NS = len(subs)
        for n in range(NS):
            stageA(n)
            if n > 0:
                stageC(n - 1)
            stageB(n)
        stageC(NS - 1)
        S.barrier()
        A.release(m)

    def _attn_jobs(self, jobs, KT, KT_t, V, V_t, lookahead=3):
        S = self.S
        B, BT = self.banks, self.bank_t
        PT, PT_t = self._PT, self._PT_t
        st = self._attn_state
        for job in jobs:
            nq = job["nq"]
            pair = (nq == 512)
            steps = []
            for kv in range(2):
                tl = job["tiles"]
                if pair:
                    assert len(tl) % 2 == 0
                    for i in range(0, len(tl), 2):
                        steps.append((kv, [tl[i], tl[i + 1]], i == 0, i + 2 >= len(tl)))
                else:
                    for i in range(len(tl)):
                        steps.append((kv, [tl[i]], i == 0, i == len(tl) - 1))
            slots = []
            obank = {}

            def issue_qk(j, job=job, nq=nq, steps=steps, slots=slots):
                kv, tls, first, last = steps[j]
                sp = st["s"] % 3
                st["s"] += 1
                rhs, rhs_t = job["rhs"](kv)
                big = self.bigs[sp]
                bts = [BT[2 * sp], BT[2 * sp + 1]]
                for u, (kt, mk) in enumerate(tls):
                    S.op("pe", lambda E, big=big, u=u, kv=kv, kt=kt, rhs=rhs, nq=nq: E.matmul(
                        big[:, u * 512:u * 512 + nq], KT[kv * 64:(kv + 1) * 64, kt * 128:(kt + 1) * 128], rhs, start=True, stop=True),
                        reads=[KT_t, rhs_t], writes=[bts[u]])
                w = (len(tls) - 1) * 512 + nq
                S.op("act", lambda E, sp=sp, big=big, w=w: E.activation(out=PT[sp][:, 0:w], in_=big[:, 0:w], func=AF.Exp),
                     reads=bts[:len(tls)], writes=[PT_t[sp]])
                for u, (kt, mk) in enumerate(tls):
                    if mk is not None:
                        pv_ = PT[sp][:, u * 512:u * 512 + nq].rearrange("p (g n) -> p g n", n=128)
                        S.op("pool", lambda E, pv_=pv_, mk=mk, nq=nq: E.tensor_tensor(out=pv_, in0=pv_, in1=mk.unsqueeze(1).to_broadcast([128, nq // 128, 128]),
                                                                                      op=ALU.mult),
                             reads=[PT_t[sp], self.consts_t], writes=[PT_t[sp]])
                slots.append(sp)

            def issue_pv(j, job=job, nq=nq, steps=steps, slots=slots, obank=obank):
                kv, tls, first, last = steps[j]
                if first:
                    obank[kv] = 6 + (st["o"] % 2)
                    st["o"] += 1
                ob = obank[kv]
                sp = slots[j]
                for u, (kt, mk) in enumerate(tls):
                    f_ = first and u == 0
                    l_ = last and u == len(tls) - 1
                    S.op("pe", lambda E, ob=ob, kv=kv, kt=kt, sp=sp, u=u, f_=f_, l_=l_, nq=nq: E.matmul(
                        B[ob][:, 0:nq], V[:, kt * 256 + kv * 128: kt * 256 + (kv + 1) * 128], PT[sp][:, u * 512:u * 512 + nq], start=f_, stop=l_),
                        reads=[V_t, PT_t[sp]], writes=[BT[ob]])
                if last:
                    job["fin"](kv, ob)
            n = len(steps)
            la = min(lookahead - 1, n)
            for j in range(la):
                issue_qk(j)
            for j in range(n):
                if j + la < n:
                    issue_qk(j + la)
                issue_pv(j)
            if "end" in job:
                job["end"]()

    def _load_kv(self, l, kchunk, vcol0):
        S, A, dr = self.S, self.A, self.dr
        TC = self.T + CTX
        NKT = TC // 128
        KT = A.alloc(TC, BF16); KT_t = Tl("KT")
        S.dma(self.dsem("ktld"), KT, dr["qkT_d"][:, kchunk, :], writes=[KT_t])
        V = A.alloc(NKT * 256, BF16); V_t = Tl("V")
        V3 = V.rearrange("p (k c) -> p k c", c=256)
        vd = dr["v_d"][:, vcol0:vcol0 + 256].rearrange("(k p) c -> p k c", p=128)
        for k0 in range(0, NKT, 11):
            k1 = min(NKT, k0 + 11)
            S.dma(self.dsem("vld"), V3[:, k0:k1, :], vd[:, k0:k1, :], writes=[V_t])
        self._PT = [A.alloc(1024, BF16) for _ in range(3)]
        self._PT_t = [Tl(f"PT{i}") for i in range(3)]
        self._attn_state = {"s": 0, "o": 0}
        return KT, KT_t, V, V_t, NKT

    def gattn_phase(self, l, need_ctx):
        S, A, dr = self.S, self.A, self.dr
        T = self.T
        m = A.mark()
        KT, KT_t, V, V_t, NKT = self._load_kv(l, 4, 0)
        QN = min(512, T)
        QT = [A.alloc(512, BF16) for _ in range(2)]; QT_t = [Tl("QT0"), Tl("QT1")]
        rden = [A.alloc(512, F32) for _ in range(2)]; rden_t = [Tl("rden0"), Tl("rden1")]
        OT = [A.alloc(4 * 512, BF16) for _ in range(2)]; OT_t = [Tl("OT0"), Tl("OT1")]
        B, BT = self.banks, self.bank_t
        qjobs = []
        if need_ctx:
            qjobs.append((0, CTX, [(0, None), (1, None)]))
        for qt in range(T // QN):
            qjobs.append((CTX + qt * QN, QN, [(k, None) for k in range(NKT)]))
        jobs = []
        cnt = [0, 0]
        for qi, (q0, nq, tl) in enumerate(qjobs):
            ot = OT[qi % 2]; ot_t = OT_t[qi % 2]
            for g in range(4):
                qb = (qi * 4 + g) % 2
                job = {"nq": nq, "tiles": tl}

                def pre(q0=q0, nq=nq, g=g, qb=qb):
                    S.dma(self.dsem(f"qtld{qb}"), QT[qb][:, 0:nq], dr["qkT_d"][:, g, q0:q0 + nq], writes=[QT_t[qb]])
                job["pre"] = pre
                job["rhs"] = (lambda kv, qb=qb, nq=nq: (QT[qb][kv * 64:(kv + 1) * 64, 0:nq], QT_t[qb]))

                def fin(kv, ob, g=g, nq=nq, ot=ot, ot_t=ot_t):
                    rb = cnt[0] % 2
                    cnt[0] += 1
                    S.op("dve", lambda E, rb=rb, ob=ob: E.reciprocal(out=rden[rb][64:128, 0:nq], in_=B[ob][64:128, 0:nq]), reads=[BT[ob]], writes=[rden_t[rb]])
                    S.op("dve", lambda E, rb=rb, ob=ob, kv=kv: E.tensor_tensor(out=ot[kv * 64:(kv + 1) * 64, g * nq:(g + 1) * nq], in0=B[ob][0:64, 0:nq],
                                                                               in1=rden[rb][64:128, 0:nq], op=ALU.mult),
                         reads=[BT[ob], rden_t[rb]], writes=[ot_t])
                job["fin"] = fin
                if g == 3:
                    def end(q0=q0, nq=nq, ot=ot, ot_t=ot_t, qi=qi):
                        S.dma(self.dsem(f"otst{qi % 2}"), dr["mixT_d"][:, 2:6, q0:q0 + nq], ot[:, 0:4 * nq].rearrange("p (c n) -> p c n", c=4), reads=[ot_t])
                    job["end"] = end
                jobs.append(job)
        if jobs:
            jobs[0]["pre"]()
        for j, job in enumerate(jobs):
            if j + 1 < len(jobs):
                jobs[j + 1]["pre"]()
            self._attn_jobs([job], KT, KT_t, V, V_t)
        S.barrier()
        A.release(m)

    def wattn_phase(self, l, need_ctx):
        S, A, dr = self.S, self.A, self.dr
        T = self.T
        TC = T + CTX
        m = A.mark()
        KT, KT_t, V, V_t, NKT = self._load_kv(l, 7, 256)
        QW = A.alloc(2 * TC, BF16); QW_t = Tl("QW")
        QW3 = QW.rearrange("p (c n) -> p c n", c=2)
        S.dma(self.dsem("qwld"), QW3, dr["qkT_d"][:, 5:7, :], writes=[QW_t])
        esink = A.alloc(8, F32); esink_t = Tl("esink")
        S.dma(self.dsem("sink"), esink[:, 0:4], dr["win_sink"][l:l + 1, :].partition_broadcast(128), writes=[esink_t])
        S.op("act", lambda E: E.activation(out=esink[:, 4:8], in_=esink[:, 0:4], func=AF.Exp), reads=[esink_t], writes=[esink_t])
        rd = [A.alloc(256, F32) for _ in range(2)]; rd_t = [Tl("rd0"), Tl("rd1")]
        OT = [A.alloc(256, BF16) for _ in range(2)]; OT_t = [Tl("wOT0"), Tl("wOT1")]
        B, BT = self.banks, self.bank_t
        mprev = self.consts[:, C_WM:C_WM + 128]
        mnext = self.consts[:, C_WM + 128:C_WM + 256]
        blocks = []
        if need_ctx:
            for i in range(2):
                blocks.append((i * 128, [(0, None), (1, None)]))
        nb = T // 128
        for i in range(nb):
            tl = []
            if i > 0:
                tl.append((2 + i - 1, mprev))
            tl.append((2 + i, None))
            if i < nb - 1:
                tl.append((2 + i + 1, mnext))
            tl += [(0, None), (1, None)]
            blocks.append((CTX + i * 128, tl))
        cnt = [0]
        jobs = []
        for bi_, (q0, tl) in enumerate(blocks):
            ot = OT[bi_ % 2]; ot_t = OT_t[bi_ % 2]
            job = {"nq": 256, "tiles": tl}
            job["rhs"] = (lambda kv, q0=q0: (QW3[kv * 64:(kv + 1) * 64, :, q0:q0 + 128], QW_t))

            def fin(kv, ob, ot=ot, ot_t=ot_t):
                rb = cnt[0] % 2
                cnt[0] += 1
                for g in range(2):
                    h = kv * 2 + g
                    S.op("dve", lambda E, rb=rb, ob=ob, g=g, h=h: E.tensor_scalar(out=rd[rb][64:128, g * 128:(g + 1) * 128], in0=B[ob][64:128, g * 128:(g + 1) * 128],
                                                                                 scalar1=esink[64:128, 4 + h:5 + h], scalar2=None, op0=ALU.add),
                         reads=[BT[ob], esink_t], writes=[rd_t[rb]])
                S.op("dve", lambda E, rb=rb: E.reciprocal(out=rd[rb][64:128, :], in_=rd[rb][64:128, :]), reads=[rd_t[rb]], writes=[rd_t[rb]])
                S.op("dve", lambda E, rb=rb, ob=ob, kv=kv: E.tensor_tensor(out=ot[kv * 64:(kv + 1) * 64, :], in0=B[ob][0:64, 0:256], in1=rd[rb][64:128, :], op=ALU.mult),
                     reads=[BT[ob], rd_t[rb]], writes=[ot_t])
            job["fin"] = fin

            def end(q0=q0, ot=ot, ot_t=ot_t, bi_=bi_):
                S.dma(self.dsem(f"wotst{bi_ % 2}"), dr["mixT_d"][:, 6:8, q0:q0 + 128], ot.rearrange("p (c n) -> p c n", c=2), reads=[ot_t])
            job["end"] = end
            jobs.append(job)
        self._attn_jobs(jobs, KT, KT_t, V, V_t)
        S.barrier()
        A.release(m)

    def gla_phase(self, l, need_ctx):
        S, A, dr = self.S, self.A, self.dr
        T = self.T
        TC = T + CTX
        NCH = TC // 64
        m = A.mark()
        cst = self.consts; cst_t = self.consts_t
        B, BT = self.banks, self.bank_t
        Q = slice(0, 64)
        raw = [A.alloc(1024, F32) for _ in range(2)]; raw_t = [Tl("raw0"), Tl("raw1")]
        ofl = [A.alloc(256, F32) for _ in range(2)]; ofl_t = [Tl("ofl0"), Tl("ofl1")]
        ofs_ = [A.alloc(256, F32) for _ in range(2)]; ofs_t = [Tl("ofs0"), Tl("ofs1")]
        Sst = A.alloc(256, F32); Sst_t = Tl("Sst")
        Sbf = A.alloc(256, BF16); Sbf_t = Tl("Sbf")
        gain_o = A.alloc(64, F32); gain_o_t = Tl("gain_o")
        S.dma(self.dsem("gaino"), gain_o, dr["gla_out_norm"][l:l + 1, :].partition_broadcast(128), writes=[gain_o_t])
        ex = [A.alloc(128, F32) for _ in range(3)]; ex_t = [Tl(f"ex{i}") for i in range(3)]
        dec = A.alloc(2, F32); dec_t = Tl("dec")
        qin = A.alloc(128, BF16); qin_t = Tl("qin")
        kout = A.alloc(128, BF16); kout_t = Tl("kout")
        k2 = A.alloc(128, BF16); k2_t = Tl("k2")
        vb = A.alloc(256, BF16); vb_t = Tl("vb")
        koutT = A.alloc(64, BF16); koutT_t = Tl("koutT")
        qinT = A.alloc(64, BF16); qinT_t = Tl("qinT")
        Qbd = A.alloc(256, BF16); Qbd_t = Tl("Qbd")
        ATm = A.alloc(256, BF16); ATm_t = Tl("ATm")
        dsm = A.alloc(256, F32); dsm_t = Tl("dsm")
        ob_ = A.alloc(256, F32); ob_t = Tl("ob")
        sqo = A.alloc(256, F32); sqo_t = Tl("sqo")
        s4 = A.alloc(16, F32); s4_t = Tl("s4")
        sg = A.alloc(256, F32); sg_t = Tl("sg")
        ofin = A.alloc(256, BF16); ofin_t = Tl("ofin")
        oT = [A.alloc(128, BF16) for _ in range(2)]; oT_t = [Tl("oT0"), Tl("oT1")]
        for bf_, bf_t in ((raw[0], raw_t[0]), (raw[1], raw_t[1]), (qin, qin_t), (kout, kout_t), (ofin, ofin_t)):
            S.op("pool", lambda E, bf_=bf_: E.memset(bf_, 0.0), writes=[bf_t])
        TRI = [cst[:, C_TRI + i * 128: C_TRI + (i + 1) * 128] for i in range(4)]
        GM = [cst[Q, C_GM + i * 64: C_GM + (i + 1) * 64] for i in range(2)]
        BD = cst[:, C_BD:C_BD + 4]
        idt = self.ident
        bT = B[0][:, :].bitcast(BF16)
        ofd = dr["of_d"]
        nout = 0
        for d in range(2):
            S.op("dve", lambda E: E.memset(Sst, 0.0), writes=[Sst_t])
            S.op("pool", lambda E: E.memset(Sbf, 0.0), writes=[Sbf_t])
            order = list(range(NCH)) if d == 0 else [3, 2, 1, 0] + list(range(NCH - 1, 3, -1))

            def load(i, d=d, order=order):
                ch = order[i]
                S.dma(self.dsem(f"rawld{i % 2}"), raw[i % 2][Q, :], dr["gla_d"][ch * 64:(ch + 1) * 64, :], writes=[raw_t[i % 2]])
                if d == 1 and (ch >= 4 or need_ctx):
                    S.dma(self.dsem(f"ofld{i % 2}"), ofl[i % 2][Q, :], ofd[ch * 64:(ch + 1) * 64, :], writes=[ofl_t[i % 2]])
            load(0)
            for i, ch in enumerate(order):
                if i + 1 < len(order):
                    load(i + 1)
                rw = raw[i % 2]; rw_t = raw_t[i % 2]
                g = rw[:, 768 + d * 128: 768 + (d + 1) * 128]
                S.op("pe", lambda E, g=g, d=d: E.matmul(B[7][:, 0:128], TRI[2 * d], g, start=True, stop=True), reads=[cst_t, rw_t], writes=[BT[7]])
                S.op("pe", lambda E, g=g, d=d: E.matmul(B[7][:, 128:256], TRI[2 * d + 1], g, start=True, stop=True), reads=[cst_t, rw_t], writes=[BT[7]])
                S.op("pe", lambda E, g=g: E.matmul(B[6][:, 0:2], g, self.ones_f[:, 0:2], start=True, stop=True), reads=[self.ones_t, rw_t], writes=[BT[6]])
                S.op("act", lambda E: E.activation(out=ex[0][Q, :], in_=B[7][Q, 0:128], func=AF.Exp), reads=[BT[7]], writes=[ex_t[0]])
                S.op("act", lambda E: E.activation(out=ex[1][Q, :], in_=B[7][Q, 0:128], func=AF.Exp, scale=-1.0), reads=[BT[7]], writes=[ex_t[1]])
                S.op("act", lambda E: E.activation(out=ex[2][Q, :], in_=B[7][Q, 128:256], func=AF.Exp), reads=[BT[7]], writes=[ex_t[2]])
                S.op("act", lambda E: E.activation(out=dec, in_=B[6][:, 0:2], func=AF.Exp), reads=[BT[6]], writes=[dec_t])
                S.op("dve", lambda E, rw=rw: E.scalar_tensor_tensor(out=qin[Q, :], in0=rw[Q, 0:128], scalar=32.0 ** -0.5, in1=ex[0][Q, :], op0=ALU.mult, op1=ALU.mult),
                     reads=[rw_t, ex_t[0]], writes=[qin_t])
                S.op("pool", lambda E, rw=rw: E.tensor_tensor(out=kout[Q, :], in0=rw[Q, 128:256], in1=ex[1][Q, :], op=ALU.mult), reads=[rw_t, ex_t[1]], writes=[kout_t])
                S.op("pool", lambda E, rw=rw: E.tensor_tensor(out=k2[Q, :], in0=rw[Q, 128:256], in1=ex[2][Q, :], op=ALU.mult), reads=[rw_t, ex_t[2]], writes=[k2_t])
                S.op("act", lambda E, rw=rw: E.activation(out=vb[Q, :], in_=rw[Q, 256:512], func=AF.Copy), reads=[rw_t], writes=[vb_t])
                S.op("pe", lambda E: E.transpose(out=bT[:, 0:128], in_=qin, identity=idt), reads=[qin_t, self.ident_t], writes=[BT[0]])
                S.op("pe", lambda E: E.transpose(out=bT[:, 128:256], in_=kout, identity=idt), reads=[kout_t, self.ident_t], writes=[BT[0]])
                S.op("act", lambda E: E.activation(out=koutT, in_=bT[:, 128:192], func=AF.Copy), reads=[BT[0]], writes=[koutT_t])
                S.op("act", lambda E: E.activation(out=qinT, in_=bT[:, 0:64], func=AF.Copy), reads=[BT[0]], writes=[qinT_t])
                for h in range(4):
                    eng_ = "dve" if h % 2 == 0 else "pool"
                    S.op(eng_, lambda E, h=h: E.tensor_scalar(out=Qbd[:, h * 64:(h + 1) * 64], in0=qinT, scalar1=BD[:, h:h + 1], scalar2=None, op0=ALU.mult),
                         reads=[qinT_t, cst_t], writes=[Qbd_t])
                S.op("pe", lambda E: E.matmul(B[1][Q, 0:256], koutT, Qbd, start=True, stop=True), reads=[koutT_t, Qbd_t], writes=[BT[1]])
                S.op("dve", lambda E, d=d: E.tensor_tensor(out=ATm[Q, :].rearrange("p (h t) -> p h t", h=4),
                                                          in0=B[1][Q, 0:256].rearrange("p (h t) -> p h t", h=4),
                                                          in1=GM[d].unsqueeze(1).to_broadcast([64, 4, 64]), op=ALU.mult),
                     reads=[BT[1], cst_t], writes=[ATm_t])
                for h in range(4):
                    S.op("pe", lambda E, h=h: E.matmul(B[2][Q, h * 64:(h + 1) * 64], qinT, Sbf[:, h * 64:(h + 1) * 64], start=True, stop=False),
                         reads=[qinT_t, Sbf_t], writes=[BT[2]])
                    S.op("pe", lambda E, h=h: E.matmul(B[2][Q, h * 64:(h + 1) * 64], ATm[Q, h * 64:(h + 1) * 64], vb[Q, h * 64:(h + 1) * 64],
                                                       start=False, stop=True),
                         reads=[ATm_t, vb_t], writes=[BT[2]])
                S.op("pe", lambda E: E.matmul(B[3][:, 0:256], k2[Q, :], vb[Q, :], start=True, stop=True), reads=[k2_t, vb_t], writes=[BT[3]])
                S.op("dve", lambda E: E.tensor_tensor(out=dsm.rearrange("p (h v) -> p h v", h=4), in0=B[3][:, 0:256].rearrange("p (h v) -> p h v", h=4),
                                                     in1=BD.unsqueeze(2).to_broadcast([128, 4, 64]), op=ALU.mult),
                     reads=[BT[3], cst_t], writes=[dsm_t])
                S.op("dve", lambda E: E.scalar_tensor_tensor(out=Sst, in0=Sst, scalar=dec[:, 0:1], in1=dsm, op0=ALU.mult, op1=ALU.add),
                     reads=[Sst_t, dec_t, dsm_t], writes=[Sst_t])
                S.op("act", lambda E: E.activation(out=Sbf, in_=Sst, func=AF.Copy), reads=[Sst_t], writes=[Sbf_t])
                if d == 0:
                    os_ = ofs_[i % 2]; os_t = ofs_t[i % 2]
                    S.op("act", lambda E, os_=os_: E.activation(out=os_[Q, :], in_=B[2][Q, 0:256], func=AF.Copy), reads=[BT[2]], writes=[os_t])
                    S.dma(self.dsem(f"ofst{i % 2}"), ofd[ch * 64:(ch + 1) * 64, :], os_[Q, :], reads=[os_t])
                elif ch >= 4 or need_ctx:
                    ol = ofl[i % 2]; ol_t = ofl_t[i % 2]
                    S.op("dve", lambda E, ol=ol: E.tensor_tensor(out=ob_[Q, :], in0=B[2][Q, 0:256], in1=ol[Q, :], op=ALU.add), reads=[BT[2], ol_t], writes=[ob_t])
                    S.op("act", lambda E: E.activation(out=sqo[Q, :], in_=ob_[Q, :], func=AF.Square), reads=[ob_t], writes=[sqo_t])
                    S.op("dve", lambda E: E.tensor_reduce(out=s4[Q, 0:4], in_=sqo[Q, :].rearrange("p (h v) -> p h v", h=4), axis=AX.X, op=ALU.add),
                         reads=[sqo_t], writes=[s4_t])
                    S.op("dve", lambda E: E.tensor_scalar(out=s4[Q, 4:8], in0=s4[Q, 0:4], scalar1=1.0 / 64, scalar2=EPS, op0=ALU.mult, op1=ALU.add),
                         reads=[s4_t], writes=[s4_t])
                    S.op("act", lambda E: E.activation(out=s4[Q, 8:12], in_=s4[Q, 4:8], func=AF.Sqrt), reads=[s4_t], writes=[s4_t])
                    S.op("dve", lambda E: E.reciprocal(out=s4[Q, 12:16], in_=s4[Q, 8:12]), reads=[s4_t], writes=[s4_t])
                    ob3 = ob_[Q, :].rearrange("p (h v) -> p h v", h=4)
                    S.op("dve", lambda E, ob3=ob3: E.tensor_tensor(out=ob3, in0=ob3, in1=s4[Q, 12:16].unsqueeze(2).to_broadcast([64, 4, 64]), op=ALU.mult),
                         reads=[ob_t, s4_t], writes=[ob_t])
                    S.op("pool", lambda E, ob3=ob3: E.tensor_tensor(out=ob3, in0=ob3, in1=gain_o[Q, :].unsqueeze(1).to_broadcast([64, 4, 64]), op=ALU.mult),
                         reads=[ob_t, gain_o_t], writes=[ob_t])
                    S.op("act", lambda E, rw=rw: E.activation(out=sg[Q, :], in_=rw[Q, 512:768], func=AF.Silu), reads=[rw_t], writes=[sg_t])
                    S.op("dve", lambda E: E.tensor_tensor(out=ofin[Q, :], in0=ob_[Q, :], in1=sg[Q, :], op=ALU.mult), reads=[ob_t, sg_t], writes=[ofin_t])
                    b4 = B[4][:, :].bitcast(BF16)
                    for c in range(2):
                        S.op("pe", lambda E, c=c, b4=b4: E.transpose(out=b4[:, c * 128:(c + 1) * 128], in_=ofin[:, c * 128:(c + 1) * 128], identity=idt),
                             reads=[ofin_t, self.ident_t], writes=[BT[4]])
                    ot = oT[nout % 2]; ot_t = oT_t[nout % 2]
                    S.op("act", lambda E, ot=ot, b4=b4: E.activation(out=ot.rearrange("p (c n) -> p c n", c=2), in_=b4[:, 0:256].rearrange("p (c n) -> p c n", c=2)[:, :, 0:64], func=AF.Copy), reads=[BT[4]], writes=[ot_t])
                    S.dma(self.dsem(f"glaost{nout % 2}"), dr["mixT_d"][:, 0:2, ch * 64:(ch + 1) * 64], ot.rearrange("p (c n) -> p c n", c=2), reads=[ot_t])
                    nout += 1
            S.barrier()
        A.release(m)

    def outproj_phase(self, l, src_x, src_xc, dst_x, dst_xc, need_ctx):
        S, A, dr = self.S, self.A, self.dr
        T = self.T
        m = A.mark()
        B, BT = self.banks, self.bank_t
        Wo = A.alloc(8 * D, BF16); Wo_t = Tl("Wo")
        Wo3 = Wo.rearrange("p (k n) -> p k n", k=8)
        wo = dr["mix_w_out"][l]
        S.dma(self.dsem("wold"), Wo3[:, 0:2, :], wo[0:256, :].rearrange("(j p) n -> p j n", p=128), writes=[Wo_t], q="pool")
        for g in range(4):
            S.dma(self.dsem("wold"), Wo3[0:64, 2 + g, :], wo[256 + g * 64: 256 + (g + 1) * 64, :], writes=[Wo_t], q="pool")
            S.dma(self.dsem("wold"), Wo3[64:128, 2 + g, :], wo[256 + (4 + g) * 64: 256 + (5 + g) * 64, :], writes=[Wo_t], q="pool")
        for g in range(2):
            S.dma(self.dsem("wold"), Wo3[0:64, 6 + g, :], wo[768 + g * 64: 768 + (g + 1) * 64, :], writes=[Wo_t], q="pool")
            S.dma(self.dsem("wold"), Wo3[64:128, 6 + g, :], wo[768 + (2 + g) * 64: 768 + (3 + g) * 64, :], writes=[Wo_t], q="pool")
        gB2 = A.alloc(2 * 1024, F32); gB2_t = Tl("gB2")
        for r in range(2):
            row = l * 6 + 2 + r
            S.dma(self.dsem("gld"), gB2[:, r * 1024:(r + 1) * 1024], dr["gates_d"][row:row + 1, :].partition_broadcast(128),
                  reads=[self.gateB_t[l]], writes=[gB2_t])
        NT = 2
        xts = [A.alloc(NT * 1024, F32) for _ in range(2)]; xts_t = [Tl("oxt0"), Tl("oxt1")]
        mT = [A.alloc(8 * 256, BF16) for _ in range(2)]; mT_t = [Tl("mT0"), Tl("mT1")]
        tmp = [A.alloc(512, F32) for _ in range(2)]; tmp_t = [Tl("otmp0"), Tl("otmp1")]
        tiles = []
        if need_ctx:
            tiles.append((src_xc, dst_xc, 0, 0, 2, 1))
        for i in range(T // 256):
            tiles.append((src_x, dst_x, i * 256, CTX + i * 256, 2, 0))

        def load(ti):
            src, dst, row0, tok0, nt, r = tiles[ti]
            b = ti % 2
            S.dma(self.dsem(f"oxld{b}"), xts[b][:, 0:nt * 1024].rearrange("p (t d) -> p t d", t=nt),
                  src[row0:row0 + nt * 128, :].rearrange("(t p) d -> p t d", p=128), writes=[xts_t[b]])
            S.dma(self.dsem(f"omld{b}"), mT[b].rearrange("p (c n) -> p c n", c=8), dr["mixT_d"][:, :, tok0:tok0 + nt * 128], writes=[mT_t[b]])
        load(0)
        on = 0
        for ti in range(len(tiles)):
            src, dst, row0, tok0, nt, r = tiles[ti]
            if ti + 1 < len(tiles):
                load(ti + 1)
            b = ti % 2
            N = nt * 128
            xt, xt_t = xts[b], xts_t[b]
            for s_ in range(nt):
                for hf in range(2):
                    bi = 6 + (on % 2)
                    tb = on % 2
                    on += 1
                    for k in range(8):
                        S.op("pe", lambda E, bi=bi, k=k, s_=s_, hf=hf, b=b: E.matmul(B[bi][:, :], mT[b][:, k * N + s_ * 128: k * N + (s_ + 1) * 128],
                                                                                    Wo3[:, k, hf * 512:(hf + 1) * 512], start=(k == 0), stop=(k == 7)),
                             reads=[mT_t[b], Wo_t], writes=[BT[bi]])
                    gB = gB2[:, r * 1024 + hf * 512: r * 1024 + (hf + 1) * 512]
                    S.op("dve", lambda E, bi=bi, tb=tb, gB=gB: E.tensor_tensor(out=tmp[tb], in0=B[bi][:, :], in1=gB, op=ALU.mult),
                         reads=[BT[bi], gB2_t], writes=[tmp_t[tb]])
                    xsl = xt[:, s_ * 1024 + hf * 512: s_ * 1024 + (hf + 1) * 512]
                    S.op("pool", lambda E, xsl=xsl, tb=tb: E.tensor_tensor(out=xsl, in0=tmp[tb], in1=xsl, op=ALU.add),
                         reads=[tmp_t[tb], xt_t], writes=[xt_t])
            S.dma(self.dsem(f"oxst{b}"), dst[row0:row0 + nt * 128, :].rearrange("(t p) d -> p t d", p=128),
                  xt[:, 0:nt * 1024].rearrange("p (t d) -> p t d", t=nt), reads=[xt_t])
        S.barrier()
        A.release(m)


_NC_CACHE = {}


def _get_nc(cfg_key, cfg):
    if cfg_key not in _NC_CACHE:
        _NC_CACHE[cfg_key] = build_nc(cfg)
    return _NC_CACHE[cfg_key]


def make_consts():
    c = np.zeros((128, NCONST), np.float32)
    c[:, C_ID:C_ID + 128] = np.eye(128, dtype=np.float32)
    s_ = np.arange(128)[:, None]
    t_ = np.arange(128)[None, :]
    same = (s_ // 64) == (t_ // 64)
    c[:, C_TRI + 0:C_TRI + 128] = (same & (s_ <= t_))
    c[:, C_TRI + 128:C_TRI + 256] = (same & (s_ > t_))
    c[:, C_TRI + 256:C_TRI + 384] = (same & (s_ >= t_))
    c[:, C_TRI + 384:C_TRI + 512] = (same & (s_ < t_))
    c[:, C_IND + 0] = (np.arange(128) < 64)
    c[:, C_IND + 1] = (np.arange(128) >= 64)
    c[:, C_WM:C_WM + 128] = (s_ >= t_)
    c[:, C_WM + 128:C_WM + 256] = (s_ <= t_)
    t64 = np.arange(64)[None, :]
    c[:, C_GM:C_GM + 64] = ((s_ % 64) <= t64)
    c[:, C_GM + 64:C_GM + 128] = ((s_ % 64) >= t64)
    c[:, C_BD:C_BD + 4] = ((np.arange(128)[:, None] // 32) == np.arange(4)[None, :])
    return c


def make_rope(T):
    t = np.arange(T)
    row = (t // 64).astype(np.float32)
    col = (t % 64).astype(np.float32)
    inv = np.power(np.float32(10000.0), -np.arange(16, dtype=np.float32) / np.float32(16)).astype(np.float32)
    ang = np.concatenate([row[:, None] * inv, col[:, None] * inv], axis=-1).astype(np.float32)
    return np.concatenate([np.cos(ang), np.sin(ang)], axis=-1).astype(np.float32)


WEIGHT_KEYS = ("mod_w", "mod_b", "norm_ffn1", "ffn1_w_in", "ffn1_w_out", "norm_ffn2", "ffn2_w_in", "ffn2_w_out",
               "norm_mix", "mix_w_in", "mix_w_out", "gla_wg_f", "gla_bg_f", "gla_wg_b", "gla_bg_b", "gla_out_norm",
               "glb_q_norm", "glb_k_norm", "win_q_norm", "win_k_norm", "win_sink")


def make_in_maps(inputs, T, depth):
    f = lambda a: np.ascontiguousarray(np.asarray(a, dtype=np.float32))
    x = f(inputs["x"]); c = f(inputs["c"]); ctx = f(inputs["ctx"]); c_ctx = f(inputs["c_ctx"])
    shared = {}
    for k in WEIGHT_KEYS:
        shared[k] = f(inputs[k])[:depth]
    shared["consts"] = make_consts()
    shared["rope"] = make_rope(T)
    maps = []
    for b in range(x.shape[0]):
        m = dict(shared)
        m["x"] = np.ascontiguousarray(x[b, :T])
        m["ctx"] = ctx[b]
        m["cvec"] = np.stack([c[b], c_ctx], axis=0)
        maps.append(m)
    return maps


def kernel(**inputs):
    cfg = Cfg()
    nc = _get_nc("full", cfg)
    maps = make_in_maps(inputs, cfg.T, cfg.depth)
    res = run_bass_kernel_spmd(nc, maps, core_ids=list(range(8)))
    return np.stack([r["out"] for r in res.results], axis=0)
```
